# Optimizing a Trainium2 kernel written in Bass

```python
import jax, jax.numpy as jnp
from jax import lax
import numpy as np

D_MODEL = 2048
BATCH = 8
SEQ = 2048
DEPTH = 2

PLE_DIM = 256
LN_EPS = 1e-5
CONV_CH = D_MODEL // 2
CONV_WIDTH = 31
ATTN_HEADS = 8
ATTN_HEAD_DIM = 128
ATTN_WIDTH = ATTN_HEADS * ATTN_HEAD_DIM
ROPE_DIM = ATTN_HEAD_DIM // 4
ROPE_THETA = 500000.0
MOBA_BLOCK = 256
MOBA_TOPK = 3
MOBA_QUERY_CHUNK = 16
MLSTM_HEADS = 4
MLSTM_WIDTH = D_MODEL // 2
MLSTM_HEAD_DIM = MLSTM_WIDTH // MLSTM_HEADS
MLSTM_QK_CONV = 4
MLSTM_CHUNK = 64
FFN_HIDDEN = -(-8 * D_MODEL // (3 * 256)) * 256
DN_ALPHA = (2 * DEPTH) ** 0.25
DN_BETA = (8 * DEPTH) ** -0.25
IN_SIZES = (CONV_CH, CONV_CH, ATTN_WIDTH, ATTN_WIDTH, ATTN_WIDTH,
            2 * MLSTM_WIDTH, MLSTM_WIDTH, 2 * MLSTM_HEADS, MLSTM_WIDTH, 3 * D_MODEL)
IN_WIDTH = sum(IN_SIZES)

kernel_name = "hybrid_conv_moba_mlstm_deepnorm"


def layer_norm(x, g, b):
    xf = x.astype(jnp.float32)
    mu = jnp.mean(xf, -1, keepdims=True)
    var = jnp.mean(jnp.square(xf - mu), -1, keepdims=True)
    y = (xf - mu) * lax.rsqrt(var + LN_EPS)
    return (y * g.astype(jnp.float32) + b.astype(jnp.float32)).astype(x.dtype)


def causal_depthwise_conv(x, w, b):
    width, ch = w.shape
    y = lax.conv_general_dilated(x, w[:, None, :].astype(x.dtype), window_strides=(1,),
                                 padding=[(width - 1, 0)],
                                 dimension_numbers=('NWC', 'WIO', 'NWC'),
                                 feature_group_count=ch)
    return y + b


def partial_rotary(x, positions):
    half = ROPE_DIM // 2
    inv_freq = ROPE_THETA ** (-jnp.arange(half, dtype=jnp.float32) / half)
    ang = positions.astype(jnp.float32)[..., None] * inv_freq
    cos = jnp.cos(ang)[:, :, None, :]
    sin = jnp.sin(ang)[:, :, None, :]
    xr = x[..., :ROPE_DIM].astype(jnp.float32)
    x1, x2 = xr[..., :half], xr[..., half:]
    rot = jnp.concatenate([x1 * cos - x2 * sin, x2 * cos + x1 * sin], -1).astype(x.dtype)
    return jnp.concatenate([rot, x[..., ROPE_DIM:]], -1)


def moba_attention(q, k, v):
    B, H, S, Dh = q.shape
    nb = -(-S // MOBA_BLOCK)
    s_pad = nb * MOBA_BLOCK
    pad = [(0, 0), (0, 0), (0, s_pad - S), (0, 0)]
    q = jnp.pad(q * (Dh ** -0.5), pad)
    k = jnp.pad(k, pad)
    v = jnp.pad(v, pad)
    kb = k.reshape(B, H, nb, MOBA_BLOCK, Dh)
    vb = v.reshape(B, H, nb, MOBA_BLOCK, Dh)
    k_mean = jnp.mean(kb.astype(jnp.float32), axis=3).astype(k.dtype)
    n_sel = min(MOBA_TOPK, nb - 1)
    n_chunks = s_pad // MOBA_QUERY_CHUNK
    qc_all = jnp.moveaxis(q.reshape(B, H, n_chunks, MOBA_QUERY_CHUNK, Dh), 2, 0)
    bidx = jnp.arange(B)[:, None, None, None]
    hidx = jnp.arange(H)[None, :, None, None]

    def one_chunk(args):
        q_c, c = args
        start = c * MOBA_QUERY_CHUNK
        blk = start // MOBA_BLOCK
        q_pos = start + jnp.arange(MOBA_QUERY_CHUNK)
        k_pos = blk * MOBA_BLOCK + jnp.arange(MOBA_BLOCK)
        k_own = lax.dynamic_index_in_dim(kb, blk, axis=2, keepdims=False)
        v_own = lax.dynamic_index_in_dim(vb, blk, axis=2, keepdims=False)
        s_own = jnp.einsum('bhqd,bhkd->bhqk', q_c, k_own).astype(jnp.float32)
        s_own = jnp.where(k_pos[None, :] <= q_pos[:, None], s_own, -jnp.inf)
        if n_sel == 0:
            p_own = jax.nn.softmax(s_own, axis=-1).astype(v.dtype)
            return jnp.einsum('bhqk,bhkd->bhqd', p_own, v_own)
        gate = jnp.einsum('bhqd,bhnd->bhqn', q_c, k_mean).astype(jnp.float32)
        gate = jnp.where(jnp.arange(nb) < blk, gate, -jnp.inf)
        _, sel = lax.top_k(gate, n_sel)
        sel_ok = jnp.arange(n_sel) < blk
        k_sel = kb[bidx, hidx, sel]
        v_sel = vb[bidx, hidx, sel]
        s_sel = jnp.einsum('bhqd,bhqnkd->bhqnk', q_c, k_sel).astype(jnp.float32)
        s_sel = jnp.where(sel_ok[:, None], s_sel, -jnp.inf)
        Bq, Hq, QC = s_sel.shape[:3]
        s_all = jnp.concatenate([s_sel.reshape(Bq, Hq, QC, n_sel * MOBA_BLOCK), s_own], -1)
        p = jax.nn.softmax(s_all, axis=-1).astype(v.dtype)
        p_sel = p[..., :n_sel * MOBA_BLOCK].reshape(Bq, Hq, QC, n_sel, MOBA_BLOCK)
        p_own = p[..., n_sel * MOBA_BLOCK:]
        return (jnp.einsum('bhqnk,bhqnkd->bhqd', p_sel, v_sel)
                + jnp.einsum('bhqk,bhkd->bhqd', p_own, v_own))

    outs = lax.map(one_chunk, (qc_all, jnp.arange(n_chunks)))
    return jnp.moveaxis(outs, 0, 2).reshape(B, H, s_pad, Dh)[:, :, :S]


def mlstm_chunkwise(q, k, v, i_pre, f_pre):
    B, H, S, Dh = q.shape
    L = MLSTM_CHUNK
    nc = S // L
    f32 = jnp.float32
    k = k * (Dh ** -0.5)
    logf = jax.nn.log_sigmoid(f_pre.astype(f32))
    ig = i_pre.astype(f32)

    def chunks(a):
        return jnp.moveaxis(a.reshape(B, H, nc, L, *a.shape[3:]), 2, 0)

    causal = jnp.tril(jnp.ones((L, L), dtype=bool))

    def step(carry, xs):
        C, n, m = carry
        qc, kc, vc, lf, ic = xs
        qf, kf, vf = qc.astype(f32), kc.astype(f32), vc.astype(f32)
        b = jnp.cumsum(lf, axis=-1)
        log_d = jnp.where(causal, b[..., :, None] - b[..., None, :] + ic[..., None, :], -jnp.inf)
        log_inter = b + m[..., None]
        m_t = jnp.maximum(log_inter, jnp.max(log_d, -1))
        d = jnp.exp(log_d - m_t[..., None])
        inter = jnp.exp(log_inter - m_t)
        s = jnp.einsum('bhtd,bhsd->bhts', qf, kf) * d
        num = (jnp.einsum('bhts,bhsd->bhtd', s, vf)
               + inter[..., None] * jnp.einsum('bhtk,bhkv->bhtv', qf, C))
        den = jnp.sum(s, -1) + inter * jnp.einsum('bhtk,bhk->bht', qf, n)
        h = num / jnp.maximum(jnp.abs(den), jnp.exp(-m_t))[..., None]
        b_last = b[..., -1]
        log_w = b_last[..., None] - b + ic
        m_new = jnp.maximum(b_last + m, jnp.max(log_w, -1))
        w = jnp.exp(log_w - m_new[..., None])
        decay = jnp.exp(b_last + m - m_new)
        C_new = decay[..., None, None] * C + jnp.einsum('bhs,bhsk,bhsv->bhkv', w, kf, vf)
        n_new = decay[..., None] * n + jnp.einsum('bhs,bhsk->bhk', w, kf)
        return (C_new, n_new, m_new), h

    init = (jnp.zeros((B, H, Dh, Dh), f32), jnp.zeros((B, H, Dh), f32), jnp.zeros((B, H), f32))
    _, hs = lax.scan(step, init, (chunks(q), chunks(k), chunks(v), chunks(logf), chunks(ig)))
    return jnp.moveaxis(hs, 0, 2).reshape(B, H, S, Dh).astype(q.dtype)


def hybrid_mixer(x, positions, w_in, b_in, conv_w, conv_b, conv_ln_g, conv_ln_b,
                 mconv_w, mconv_b, w_br_conv, w_br_attn, w_br_mlstm, w_out):
    B, S, D = x.shape
    z = x @ w_in + b_in
    offs = np.cumsum(IN_SIZES)[:-1].tolist()
    c_val, c_gate, q_a, k_a, v_a, qk_m, v_m, if_m, o_m, gates = jnp.split(z, offs, axis=-1)

    u = c_val * jax.nn.sigmoid(c_gate)
    u = causal_depthwise_conv(u, conv_w, conv_b)
    u = jax.nn.silu(layer_norm(u, conv_ln_g, conv_ln_b))
    y_conv = u @ w_br_conv

    def attn_heads(t):
        return t.reshape(B, S, ATTN_HEADS, ATTN_HEAD_DIM)
    qa = partial_rotary(attn_heads(q_a), positions).transpose(0, 2, 1, 3)
    ka = partial_rotary(attn_heads(k_a), positions).transpose(0, 2, 1, 3)
    va = attn_heads(v_a).transpose(0, 2, 1, 3)
    o = moba_attention(qa, ka, va).transpose(0, 2, 1, 3).reshape(B, S, ATTN_WIDTH)
    y_attn = o @ w_br_attn

    qk = jax.nn.silu(causal_depthwise_conv(qk_m, mconv_w, mconv_b))
    qm, km = jnp.split(qk, 2, axis=-1)
    def m_heads(t):
        return t.reshape(B, S, MLSTM_HEADS, MLSTM_HEAD_DIM).transpose(0, 2, 1, 3)
    i_pre = if_m[..., :MLSTM_HEADS].transpose(0, 2, 1)
    f_pre = if_m[..., MLSTM_HEADS:].transpose(0, 2, 1)
    h = mlstm_chunkwise(m_heads(qm), m_heads(km), m_heads(v_m), i_pre, f_pre)
    h = h.transpose(0, 2, 1, 3).reshape(B, S, MLSTM_WIDTH) * jax.nn.sigmoid(o_m)
    y_mlstm = h @ w_br_mlstm

    g = jax.nn.sigmoid(gates).reshape(B, S, 3, D)
    merged = g[:, :, 0] * y_conv + g[:, :, 1] * y_attn + g[:, :, 2] * y_mlstm
    return merged @ w_out


def setup_inputs(seed: int = 0) -> dict:
    key = jax.random.key(seed)
    ks = jax.random.split(key, 32)
    f32 = jnp.float32

    def nrm(k, shape, fan_in, scale=1.0):
        return jax.random.normal(k, shape, f32) * (scale * fan_in ** -0.5)

    def gain(k, n):
        return 1.0 + 0.02 * jax.random.normal(k, (DEPTH, n), f32)

    def bias(k, n):
        return 0.02 * jax.random.normal(k, (DEPTH, n), f32)

    x = jax.random.normal(ks[0], (BATCH, SEQ, D_MODEL), f32)
    p = jax.random.normal(ks[1], (DEPTH, BATCH, SEQ, PLE_DIM), f32)
    positions = (jax.random.randint(ks[2], (BATCH, 1), 0, 1024, dtype=jnp.int32)
                 + jnp.arange(SEQ, dtype=jnp.int32)[None, :])
    w_in = nrm(ks[3], (DEPTH, D_MODEL, IN_WIDTH), D_MODEL)
    f_off = sum(IN_SIZES[:7]) + MLSTM_HEADS
    b_in = bias(ks[4], IN_WIDTH).at[:, f_off:f_off + MLSTM_HEADS].add(
        jnp.linspace(3.0, 6.0, MLSTM_HEADS, dtype=f32))
    conv_w = nrm(ks[5], (DEPTH, CONV_WIDTH, CONV_CH), CONV_WIDTH)
    conv_b = bias(ks[6], CONV_CH)
    conv_ln_g = gain(ks[7], CONV_CH)
    conv_ln_b = bias(ks[8], CONV_CH)
    mconv_w = nrm(ks[9], (DEPTH, MLSTM_QK_CONV, 2 * MLSTM_WIDTH), MLSTM_QK_CONV)
    mconv_b = bias(ks[10], 2 * MLSTM_WIDTH)
    w_br_conv = nrm(ks[11], (DEPTH, CONV_CH, D_MODEL), CONV_CH)
    w_br_attn = nrm(ks[12], (DEPTH, ATTN_WIDTH, D_MODEL), ATTN_WIDTH)
    w_br_mlstm = nrm(ks[13], (DEPTH, MLSTM_WIDTH, D_MODEL), MLSTM_WIDTH)
    w_out = nrm(ks[14], (DEPTH, D_MODEL, D_MODEL), D_MODEL, DN_BETA)
    ln_mix_g = gain(ks[15], D_MODEL)
    ln_mix_b = bias(ks[16], D_MODEL)
    w_ffn_gate = nrm(ks[17], (DEPTH, D_MODEL, FFN_HIDDEN), D_MODEL)
    w_ffn_up = nrm(ks[18], (DEPTH, D_MODEL, FFN_HIDDEN), D_MODEL)
    w_ffn_down = nrm(ks[19], (DEPTH, FFN_HIDDEN, D_MODEL), FFN_HIDDEN, DN_BETA)
    ln_ffn_g = gain(ks[20], D_MODEL)
    ln_ffn_b = bias(ks[21], D_MODEL)
    w_ple_gate = nrm(ks[22], (DEPTH, D_MODEL, D_MODEL), D_MODEL)
    w_ple_proj = nrm(ks[23], (DEPTH, PLE_DIM, D_MODEL), PLE_DIM, DN_BETA)
    ln_ple_g = gain(ks[24], D_MODEL)
    ln_ple_b = bias(ks[25], D_MODEL)
    return {"x": x, "p": p, "positions": positions, "w_in": w_in, "b_in": b_in,
            "conv_w": conv_w, "conv_b": conv_b, "conv_ln_g": conv_ln_g, "conv_ln_b": conv_ln_b,
            "mconv_w": mconv_w, "mconv_b": mconv_b, "w_br_conv": w_br_conv,
            "w_br_attn": w_br_attn, "w_br_mlstm": w_br_mlstm, "w_out": w_out,
            "ln_mix_g": ln_mix_g, "ln_mix_b": ln_mix_b, "w_ffn_gate": w_ffn_gate,
            "w_ffn_up": w_ffn_up, "w_ffn_down": w_ffn_down, "ln_ffn_g": ln_ffn_g,
            "ln_ffn_b": ln_ffn_b, "w_ple_gate": w_ple_gate, "w_ple_proj": w_ple_proj,
            "ln_ple_g": ln_ple_g, "ln_ple_b": ln_ple_b}


def reference(x, p, positions, w_in, b_in, conv_w, conv_b, conv_ln_g, conv_ln_b,
              mconv_w, mconv_b, w_br_conv, w_br_attn, w_br_mlstm, w_out,
              ln_mix_g, ln_mix_b, w_ffn_gate, w_ffn_up, w_ffn_down, ln_ffn_g, ln_ffn_b,
              w_ple_gate, w_ple_proj, ln_ple_g, ln_ple_b):
    for i in range(DEPTH):
        mix = hybrid_mixer(x, positions, w_in[i], b_in[i], conv_w[i], conv_b[i],
                           conv_ln_g[i], conv_ln_b[i], mconv_w[i], mconv_b[i],
                           w_br_conv[i], w_br_attn[i], w_br_mlstm[i], w_out[i])
        x = layer_norm(DN_ALPHA * x + mix, ln_mix_g[i], ln_mix_b[i])
        ffn = (jax.nn.silu(x @ w_ffn_gate[i]) * (x @ w_ffn_up[i])) @ w_ffn_down[i]
        x = layer_norm(DN_ALPHA * x + ffn, ln_ffn_g[i], ln_ffn_b[i])
        ple = jax.nn.sigmoid(x @ w_ple_gate[i]) * (p[i] @ w_ple_proj[i])
        x = layer_norm(DN_ALPHA * x + ple, ln_ple_g[i], ln_ple_b[i])
    return x
```

```python
import math
from contextlib import ExitStack

import numpy as np
import concourse.bass as bass
import concourse.mybir as mybir
from concourse.bass_utils import run_bass_kernel_spmd

F32 = mybir.dt.float32
BF16 = mybir.dt.bfloat16
I32 = mybir.dt.int32
AF = mybir.ActivationFunctionType
ALU = mybir.AluOpType
AX = mybir.AxisListType

D = 2048
T = 2048
NT = 16
FF = 5632
NFT = 44
INW = 15368
DEPTH = 2
ALPHA = (2 * DEPTH) ** 0.25
EPS = 1e-5
NEG = -30000.0
SAME_ENGINE_SYNC = True
O_CV, O_CG, O_Q, O_K, O_V, O_QKM, O_VM, O_I, O_F, O_OM, O_G = 0, 1024, 2048, 3072, 4096, 5120, 7168, 8192, 8196, 8200, 9224

CV_SEGS = [("b_cv", 8), ("b_cg", 8), ("b_q", 8), ("b_k", 8), ("b_v", 8), ("b_qkm", 16), ("b_vm", 8), ("b_i", 1),
           ("b_f", 1), ("b_om", 8), ("b_g", 48), ("conv_w", 248), ("conv_b", 8), ("cln_g", 8), ("cln_b", 8),
           ("mconv_w", 64), ("mconv_b", 16), ("ln_mix_g", 16), ("ln_mix_b", 16), ("ln_ffn_g", 16),
           ("ln_ffn_b", 16), ("ln_ple_g", 16), ("ln_ple_b", 16)]
CV_OFF = {}
_o = 0
for _n, _c in CV_SEGS:
    CV_OFF[_n] = _o
    _o += _c
CV_L = _o
C_ID, C_TRI, C_ONES, C_HSEL, C_GMASK, C_INVF, C_BSEL = 0, 128, 256, 384, 896, 960, 961
C_N = C_BSEL + 1024


def _tiles(v):
    n = v.shape[0]
    return np.ascontiguousarray(v.reshape(n // 128, 128).T)


def _pack_cvec(inp):
    out = np.zeros((128, DEPTH * CV_L), np.float32)
    for l in range(DEPTH):
        b = inp["b_in"][l]
        segs = {
            "b_cv": _tiles(b[O_CV:O_CG]), "b_cg": _tiles(b[O_CG:O_Q]), "b_q": _tiles(b[O_Q:O_K]),
            "b_k": _tiles(b[O_K:O_V]), "b_v": _tiles(b[O_V:O_QKM]), "b_qkm": _tiles(b[O_QKM:O_VM]),
            "b_vm": _tiles(b[O_VM:O_I]), "b_om": _tiles(b[O_OM:O_G]), "b_g": _tiles(b[O_G:INW]),
            "conv_b": _tiles(inp["conv_b"][l]), "cln_g": _tiles(inp["conv_ln_g"][l]),
            "cln_b": _tiles(inp["conv_ln_b"][l]), "mconv_b": _tiles(inp["mconv_b"][l]),
            "ln_mix_g": _tiles(inp["ln_mix_g"][l]), "ln_mix_b": _tiles(inp["ln_mix_b"][l]),
            "ln_ffn_g": _tiles(inp["ln_ffn_g"][l]), "ln_ffn_b": _tiles(inp["ln_ffn_b"][l]),
            "ln_ple_g": _tiles(inp["ln_ple_g"][l]), "ln_ple_b": _tiles(inp["ln_ple_b"][l]),
        }
        bi = np.zeros((128, 1), np.float32)
        bi[0:4, 0] = b[O_I:O_F]
        bf = np.zeros((128, 1), np.float32)
        bf[0:4, 0] = b[O_F:O_OM]
        segs["b_i"] = bi
        segs["b_f"] = bf
        cw = inp["conv_w"][l]
        segs["conv_w"] = np.ascontiguousarray(cw.reshape(31, 8, 128).transpose(2, 1, 0).reshape(128, 248))
        mw = inp["mconv_w"][l]
        segs["mconv_w"] = np.ascontiguousarray(mw.reshape(4, 16, 128).transpose(2, 1, 0).reshape(128, 64))
        for n, c in CV_SEGS:
            a = segs[n]
            assert a.shape == (128, c), (n, a.shape)
            out[:, l * CV_L + CV_OFF[n]: l * CV_L + CV_OFF[n] + c] = a
    return out


def _consts():
    c = np.zeros((128, C_N), np.float32)
    c[:, C_ID:C_ID + 128] = np.eye(128, dtype=np.float32)
    k = np.arange(128)[:, None]
    q = np.arange(128)[None, :]
    c[:, C_TRI:C_TRI + 128] = np.where(k <= q, 0.0, NEG)
    c[:, C_ONES:C_ONES + 128] = 1.0
    for h in range(4):
        c[h, C_HSEL + h * 128: C_HSEL + (h + 1) * 128] = 1.0
    gm = np.zeros((8, 8), np.float32)
    for qt in range(8):
        for j in range(8):
            gm[qt, j] = 0.0 if j < 4 + qt // 2 else -1e30
    c[:, C_GMASK:C_GMASK + 64] = gm.reshape(1, 64)
    invf = (500000.0 ** (-(np.arange(16, dtype=np.float32) / np.float32(16)))).astype(np.float32)
    c[0:16, C_INVF] = invf
    c[16:32, C_INVF] = invf
    for j in range(8):
        c[j, C_BSEL + j * 128: C_BSEL + (j + 1) * 128] = 1.0
    return c


class _Buf:
    __slots__ = ("name", "w", "r")

    def __init__(self, name):
        self.name = name
        self.w = None
        self.r = {}


class _Eng:
    def __init__(self, name, h, sem):
        self.name = name
        self.key = "E_" + name
        self.h = h
        self.sem = sem
        self.cnt = 0
        self.seen = {}
        self.pending = False


class _Sched:
    def __init__(self, nc, stack):
        self.nc = nc
        self.stack = stack
        self.eng = {}
        for name, h in (("pe", nc.tensor), ("act", nc.scalar), ("dve", nc.vector), ("pool", nc.gpsimd),
                        ("sp", nc.sync)):
            self.eng[name] = _Eng(name, h, stack.enter_context(nc.semaphore("sem_" + name)))
        self.bar_sem = stack.enter_context(nc.semaphore("sem_bar"))
        self.bar_n = 0
        self.dsem = {}
        self.bufs = []
        self.dma_out = {}
        self.nbuf = 0
        self.free_sems = {}
        self.nsem_alloc = 0
        self.store_q = "pool"

    def buf(self, name):
        self.nbuf += 1
        b = _Buf("%s#%d" % (name, self.nbuf))
        self.bufs.append(b)
        return b

    def _deps(self, e, r, w):
        toks = []
        for b in r:
            if b.w is not None:
                toks.append(b.w)
        for b in w:
            if b.w is not None:
                toks.append(b.w)
            toks.extend(b.r.values())
        for (k, sem, val) in toks:
            if k == e.key and (e.name == "pe" or not SAME_ENGINE_SYNC):
                continue
            if e.seen.get(k, 0) >= val:
                continue
            e.h.wait_ge(sem, val)
            e.seen[k] = val

    def op(self, en, fn, r=(), w=(), inc=True):
        e = self.eng[en]
        self._deps(e, r, w)
        ins = fn()
        if inc:
            ins.then_inc(e.sem, 1)
            e.cnt += 1
            tok = (e.key, e.sem, e.cnt)
            e.pending = False
        else:
            assert en == "pe"
            tok = (e.key, e.sem, e.cnt + 1)
            e.pending = True
        for b in r:
            b.r[e.key] = tok
        for b in w:
            b.w = tok
            b.r = {}
        return tok

    def dma(self, pairs, r=(), w=(), key=None, q=None, **kw):
        if q is None:
            q = "sp" if w else self.store_q
        e = self.eng[q]
        self._deps(e, r, w)
        if key is None:
            key = ("W_" + w[0].name) if w else ("R_" + r[0].name)
        ent = self.dsem.get(key)
        if ent is None:
            fl = self.free_sems.setdefault(q, [])
            if fl:
                ent = fl.pop()
            else:
                self.nsem_alloc += 1
                ent = [self.stack.enter_context(self.nc.semaphore("d%d" % self.nsem_alloc)), 0, q]
            self.dsem[key] = ent
        assert ent[2] == q, (key, ent[2], q)
        if ent[1] > 0 and e.seen.get(key, 0) < ent[1]:
            e.h.wait_ge(ent[0], ent[1])
            e.seen[key] = ent[1]
        for (o, i) in pairs:
            e.h.dma_start(out=o, in_=i, **kw).then_inc(ent[0], 16)
            ent[1] += 16
        tok = (key, ent[0], ent[1])
        for b in r:
            b.r[key] = tok
        for b in w:
            b.w = tok
            b.r = {}
        self.dma_out[key] = tok
        return tok

    def barrier(self):
        sp = self.eng["sp"]
        for key, (k, sem, val) in self.dma_out.items():
            if sp.seen.get(k, 0) < val:
                sp.h.wait_ge(sem, val)
                sp.seen[k] = val
        for n in ("pe", "act", "dve", "pool"):
            e = self.eng[n]
            assert not e.pending, "pending non-inc op on " + n
            if e.cnt > sp.seen.get(e.key, 0):
                sp.h.wait_ge(e.sem, e.cnt)
                sp.seen[e.key] = e.cnt
        self.bar_n += 1
        sp.h.sem_inc(self.bar_sem, 1)
        for n in ("pe", "act", "dve", "pool"):
            e = self.eng[n]
            e.h.wait_ge(self.bar_sem, self.bar_n)
            for n2 in ("pe", "act", "dve", "pool"):
                e.seen[self.eng[n2].key] = self.eng[n2].cnt
        for b in self.bufs:
            b.w = None
            b.r = {}
        self.dma_out = {}
        for key, ent in self.dsem.items():
            self.free_sems.setdefault(ent[2], []).append(ent)
        self.dsem = {}
        for e in self.eng.values():
            e.seen = {k: v for k, v in e.seen.items() if k.startswith("E_")}
        self.bufs = [b for b in self.bufs if not b.name.startswith("~")]


class _Ring:
    uid = 0

    def __init__(self, S, nc, stack, name, n, shape, dtype):
        _Ring.uid += 1
        self.t = [stack.enter_context(nc.sbuf_tensor("%s_r%d_%d" % (name, _Ring.uid, i), list(shape), dtype))
                  for i in range(n)]
        self.b = [S.buf(name) for i in range(n)]
        self.n = n
        self.i = 0

    def next(self):
        i = self.i % self.n
        self.i += 1
        return self.t[i], self.b[i]


def build(n_layers=DEPTH, stop=None, dbg=()):
    nc = bass.Bass("TRN2", target_bir_lowering=False)

    def din(name, shape, dt=F32):
        return nc.dram_tensor(name, list(shape), dt, kind="ExternalInput").ap()

    def dscr(name, shape, dt):
        kind = "ExternalOutput" if name in dbg else "Internal"
        return nc.dram_tensor(name, list(shape), dt, kind=kind).ap()

    xT_in = din("xT", [D, T])
    pT_in = din("pT", [DEPTH, 256, T])
    pos_in = din("pos", [1, T], I32)
    w_in = din("w_in", [DEPTH, D, INW])
    w_brc = din("w_br_conv", [DEPTH, 1024, D])
    w_bra = din("w_br_attn", [DEPTH, 1024, D])
    w_brm = din("w_br_mlstm", [DEPTH, 1024, D])
    w_out = din("w_out", [DEPTH, D, D])
    w_fg = din("w_ffn_gate", [DEPTH, D, FF])
    w_fu = din("w_ffn_up", [DEPTH, D, FF])
    w_fd = din("w_ffn_down", [DEPTH, FF, D])
    w_pg = din("w_ple_gate", [DEPTH, D, D])
    w_pp = din("w_ple_proj", [DEPTH, 256, D])
    cvec_in = din("cvec", [128, DEPTH * CV_L])
    cst_in = din("cst", [128, C_N])
    out_T = nc.dram_tensor("outT", [D, T], F32, kind="ExternalOutput").ap()

    xbfd = dscr("xbfd", [D, T], BF16)
    x32d = dscr("x32d", [D, T], F32)
    r32d = dscr("r32d", [D, T], F32)
    brT = dscr("brT", [3072, T], BF16)
    kTd = dscr("kTd", [1024, T], BF16)
    vTd = dscr("vTd", [1024, T], BF16)
    qTd = dscr("qTd", [1024, T], BF16)
    q32d = dscr("q32d", [1024, 1024], F32)
    qkmd = dscr("qkmd", [2048, T], BF16)
    vmd = dscr("vmd", [1024, T], BF16)
    sgod = dscr("sgod", [1024, T], BF16)
    sgated = dscr("sgated", [6144, T], BF16)
    mrgd = dscr("mrgd", [D, T], BF16)
    ffd = dscr("ffd", [FF, T], BF16)
    kmd = dscr("kmd", [128, 64], F32)
    rotd = dscr("rotd", [2, 32, T], F32)
    gated = dscr("gated", [3, 4, T], F32)

    with ExitStack() as G:
        S = _Sched(nc, G)
        _uid = [0]

        def sb(st, name, shape, dt):
            _uid[0] += 1
            return st.enter_context(nc.sbuf_tensor("%s_u%d" % (name, _uid[0]), list(shape), dt))

        cv = sb(G, "cv", [128, DEPTH * CV_L], F32)
        cst = sb(G, "cst", [128, C_N], F32)
        cbf = sb(G, "cbf", [128, 384 + 1024], BF16)
        b_cv, b_cst, b_cbf = S.buf("cv"), S.buf("cst"), S.buf("cbf")
        S.dma([(cv[:], cvec_in[:, :])], w=[b_cv])
        S.dma([(cst[:], cst_in[:, :])], w=[b_cst])
        S.op("dve", lambda: nc.vector.tensor_copy(out=cbf[:, 0:384], in_=cst[:, 0:384]), r=[b_cst], w=[b_cbf])
        S.op("dve", lambda: nc.vector.tensor_copy(out=cbf[:, 384:1408], in_=cst[:, C_BSEL:C_BSEL + 1024]),
             r=[b_cst], w=[b_cbf])
        ident_bf = cbf[:, 0:128]
        tri_bf = cbf[:, 128:256]
        ones_bf = cbf[:, 256:384]
        ident32 = cst[:, C_ID:C_ID + 128]
        tri32 = cst[:, C_TRI:C_TRI + 128]
        ones32 = cst[:, C_ONES:C_ONES + 128]
        ps = G.enter_context(nc.psum_tensor("ps", [128, 8, 512], F32))
        psf = ps[:].rearrange("p b n -> p (b n)")
        S.barrier()

        def cvc(l, name, i=0, n=1):
            o = l * CV_L + CV_OFF[name] + i
            return cv[:, o:o + n]

        CONSTS = [b_cv, b_cst, b_cbf]

        class WStream:
            def __init__(self, st, nslots=3):
                self.stg = _Ring(S, nc, st, "wst", nslots, [128, 16, 128], F32)
                self.wbf = _Ring(S, nc, st, "wbf", nslots, [128, 16, 128], BF16)
                self.units = []
                self.ld = 0
                self.cs = 0
                self.stage = {}
                self.res = {}
                self.ncast = 0
                self.keys = []
                self.cursor = 0

            def add(self, src2d, nkt, M=128, dst=None, key=None):
                self.units.append((src2d, nkt, M, dst))
                self.keys.append(key)
                return len(self.units) - 1

            def take(self, key):
                i = self.cursor
                assert self.keys[i] == key, (i, self.keys[i], key)
                self.cursor += 1
                return i

            def _load(self, u):
                src2d, nkt, M, dst = self.units[u]
                t, b = self.stg.next()
                src = src2d.rearrange("(kt p) m -> p kt m", p=128)
                pairs = []
                for k0 in range(0, nkt, 4):
                    k1 = min(nkt, k0 + 4)
                    pairs.append((t[:, k0:k1, 0:M], src[:, k0:k1, :]))
                S.dma(pairs, w=[b])
                self.stage[u] = (t, b)

            def _cast(self, u):
                src2d, nkt, M, dst = self.units[u]
                t, b = self.stage.pop(u)
                if dst is None:
                    o, ob = self.wbf.next()
                    oap = o[:, 0:nkt, 0:M]
                    self.res[u] = (o, ob)
                else:
                    oap, ob = dst
                    self.res[u] = (None, ob)
                self.ncast += 1
                if self.ncast % 2:
                    S.op("act", lambda: nc.scalar.copy(out=oap, in_=t[:, 0:nkt, 0:M]), r=[b], w=[ob])
                else:
                    S.op("dve", lambda: nc.vector.tensor_copy(out=oap, in_=t[:, 0:nkt, 0:M]), r=[b], w=[ob])

            def get(self, u):
                while self.ld < min(len(self.units), u + 3):
                    self._load(self.ld)
                    self.ld += 1
                while self.cs < min(len(self.units), u + 2):
                    self._cast(self.cs)
                    self.cs += 1
                return self.res.pop(u)

        def gemm_job(ws, units, nkts, rhs_fn, M, pgap, pgbuf, tok0=0, nch=4):
            total = sum(nkts)
            kg = 0
            for u, nkt in zip(units, nkts):
                wt, wb = ws.get(u)
                for k in range(nkt):
                    for c in range(nch):
                        rap, rb = rhs_fn(kg, tok0 + c * 512)
                        last = (kg == total - 1 and c == nch - 1)
                        S.op("pe", lambda wt=wt, k=k, c=c, rap=rap, kg=kg: nc.tensor.matmul(
                            pgap[0:M, c * 512:(c + 1) * 512], lhsT=wt[:, k, 0:M], rhs=rap,
                            start=(kg == 0), stop=(kg == total - 1)),
                            r=[wb, rb], w=[pgbuf], inc=last)
                    kg += 1

        PGA = [psf[:, 0:2048], psf[:, 2048:4096]]

        def load_xbf(st, from32=None):
            xbf = sb(st, "xbf", [128, NT, T], BF16)
            xb = [S.buf("xbf") for _ in range(NT)]
            if from32 is not None:
                with ExitStack() as tmpst:
                    ring = _Ring(S, nc, tmpst, "pre32", 2, [128, T], F32)
                    for n in range(NT):
                        t, b = ring.next()
                        S.dma([(t[:], from32[n * 128:(n + 1) * 128, :])], w=[b])
                        if n % 2:
                            S.op("act", lambda t=t, n=n: nc.scalar.copy(out=xbf[:, n, :], in_=t[:]), r=[b], w=[xb[n]])
                        else:
                            S.op("dve", lambda t=t, n=n: nc.vector.tensor_copy(out=xbf[:, n, :], in_=t[:]),
                                 r=[b], w=[xb[n]])
                    S.barrier()
                return xbf, xb
            src = xbfd.rearrange("(kt p) t -> p kt t", p=128)
            for g in range(NT):
                S.dma([(xbf[:, g:g + 1, :], src[:, g:g + 1, :])], w=xb[g:g + 1], key="W_xbf%d" % g)
            return xbf, xb

        def ln_finish(st, s1, s2, bs1, bs2, nfeat, ntok=T):
            mean = sb(st, "ln_mean", [128, ntok], F32)
            rstd = sb(st, "ln_rstd", [128, ntok], F32)
            nmr = sb(st, "ln_nmr", [128, ntok], F32)
            bm, br, bn = S.buf("ln_mean"), S.buf("ln_rstd"), S.buf("ln_nmr")
            pg = [S.buf("lnpg0"), S.buf("lnpg1")]
            nch = ntok // 512
            for i, (s, bs) in enumerate(((s1, bs1), (s2, bs2))):
                for c in range(nch):
                    S.op("pe", lambda i=i, c=c, s=s: nc.tensor.matmul(
                        PGA[i][:, c * 512:(c + 1) * 512], lhsT=ones32, rhs=s[:, c * 512:(c + 1) * 512],
                        start=True, stop=True), r=[bs] + CONSTS, w=[pg[i]], inc=(c == nch - 1))
            inv = 1.0 / nfeat
            S.op("act", lambda: nc.scalar.activation(out=mean[:], in_=PGA[0][:, 0:ntok], func=AF.Copy, scale=inv),
                 r=[pg[0]], w=[bm])
            S.op("dve", lambda: nc.vector.tensor_tensor(out=nmr[:], in0=mean[:], in1=mean[:], op=ALU.mult),
                 r=[bm], w=[bn])
            S.op("dve", lambda: nc.vector.scalar_tensor_tensor(out=rstd[:], in0=PGA[1][:, 0:ntok], scalar=inv,
                                                               in1=nmr[:], op0=ALU.mult, op1=ALU.subtract),
                 r=[pg[1], bn], w=[br])
            S.op("dve", lambda: nc.vector.tensor_scalar(out=rstd[:], in0=rstd[:], scalar1=0.0, scalar2=EPS,
                                                        op0=ALU.max, op1=ALU.add), r=[br], w=[br])
            S.op("act", lambda: nc.scalar.activation(out=rstd[:], in_=rstd[:], func=AF.Sqrt), r=[br], w=[br])
            S.op("dve", lambda: nc.vector.reciprocal(out=rstd[:], in_=rstd[:]), r=[br], w=[br])
            S.op("dve", lambda: nc.vector.scalar_tensor_tensor(out=nmr[:], in0=mean[:], scalar=-1.0, in1=rstd[:],
                                                               op0=ALU.mult, op1=ALU.mult), r=[bm, br], w=[bn])
            return rstd, nmr, br, bn

        def residual_ln(ph, x32_src, chunks, y_job, gname, bname, L, inner_alloc=None, final=False):
            rtr = _Ring(S, nc, ph, "rt", 2, [128, T], F32)
            s1 = sb(ph, "ls1", [128, T], F32)
            s2 = sb(ph, "ls2", [128, T], F32)
            bs1, bs2 = S.buf("ls1"), S.buf("ls2")
            S.op("pool", lambda: nc.gpsimd.memset(s1[:], 0.0), w=[bs1])
            S.op("pool", lambda: nc.gpsimd.memset(s2[:], 0.0), w=[bs2])
            inner = ph.enter_context(ExitStack())
            x32r = _Ring(S, nc, inner, "x32t", 2, [128, T], F32)
            sq = sb(inner, "lsq", [128, T], F32)
            bsq = S.buf("lsq")
            ctx = inner_alloc(inner) if inner_alloc is not None else None
            for (t0, tn) in chunks:
                for n in range(NT):
                    xt, bxt = x32r.next()
                    S.dma([(xt[:, 0:tn], x32_src[n * 128:(n + 1) * 128, t0:t0 + tn])], w=[bxt])
                    yap, ybuf = y_job(ctx, n, t0, tn)
                    rt, brt = rtr.next()
                    S.op("dve", lambda rt=rt, xt=xt, yap=yap, tn=tn: nc.vector.scalar_tensor_tensor(
                        out=rt[:, 0:tn], in0=xt[:, 0:tn], scalar=ALPHA, in1=yap, op0=ALU.mult, op1=ALU.add),
                        r=[bxt, ybuf], w=[brt])
                    S.op("act", lambda rt=rt, tn=tn: nc.scalar.activation(out=sq[:, 0:tn], in_=rt[:, 0:tn],
                                                                          func=AF.Square), r=[brt], w=[bsq])
                    S.op("dve", lambda rt=rt, t0=t0, tn=tn: nc.vector.tensor_tensor(
                        out=s1[:, t0:t0 + tn], in0=s1[:, t0:t0 + tn], in1=rt[:, 0:tn], op=ALU.add),
                        r=[brt, bs1], w=[bs1])
                    S.op("dve", lambda t0=t0, tn=tn: nc.vector.tensor_tensor(
                        out=s2[:, t0:t0 + tn], in0=s2[:, t0:t0 + tn], in1=sq[:, 0:tn], op=ALU.add),
                        r=[bsq, bs2], w=[bs2])
                    S.dma([(r32d[n * 128:(n + 1) * 128, t0:t0 + tn], rt[:, 0:tn])], r=[brt])
            S.barrier()
            inner.close()
            rstd, nmr, brs, bnm = ln_finish(ph, s1, s2, bs1, bs2, float(D))
            xor_ = _Ring(S, nc, ph, "xo", 3, [128, T], F32)
            xbr = _Ring(S, nc, ph, "xb16", 3, [128, T], BF16)
            rt2r = _Ring(S, nc, ph, "rt2", 4, [128, T], F32)
            for n in range(NT):
                rt, brt = rtr.next() if n % 3 == 0 else rt2r.next()
                S.dma([(rt[:], r32d[n * 128:(n + 1) * 128, :])], w=[brt])
                S.op("dve", lambda rt=rt: nc.vector.tensor_tensor(out=rt[:], in0=rt[:], in1=rstd[:], op=ALU.mult),
                     r=[brt, brs], w=[brt])
                S.op("dve", lambda rt=rt: nc.vector.tensor_tensor(out=rt[:], in0=rt[:], in1=nmr[:], op=ALU.add),
                     r=[brt, bnm], w=[brt])
                xo, bxo = xor_.next()
                S.op("act", lambda rt=rt, xo=xo, n=n: nc.scalar.activation(
                    out=xo[:], in_=rt[:], func=AF.Identity, bias=cvc(L, bname, n), scale=cvc(L, gname, n)),
                    r=[brt, b_cv], w=[bxo])
                if final:
                    S.dma([(out_T[n * 128:(n + 1) * 128, :], xo[:])], r=[bxo])
                else:
                    S.dma([(x32d[n * 128:(n + 1) * 128, :], xo[:])], r=[bxo])
                    xb16, bxb = xbr.next()
                    S.op("act", lambda xo=xo, xb16=xb16: nc.scalar.copy(out=xb16[:], in_=xo[:]),
                         r=[bxo], w=[bxb])
                    S.dma([(xbfd[n * 128:(n + 1) * 128, :], xb16[:])], r=[bxb])
            S.barrier()

        WS = WStream(G)
        wbs_all = (w_brc, w_bra, w_brm)
        for L_ in range(n_layers):
            plan = []
            for c in range(8):
                plan += [(O_CV + c * 128, 128), (O_CG + c * 128, 128)]
            for h in range(8):
                plan += [(O_K + h * 128, 128), (O_V + h * 128, 128), (O_Q + h * 128, 128)]
            plan += [(O_QKM + c * 128, 128) for c in range(16)]
            plan += [(O_VM + c * 128, 128) for c in range(8)]
            plan += [(O_OM + c * 128, 128) for c in range(8)]
            plan += [(O_I, 4), (O_F, 4)]
            plan += [(O_G + c * 128, 128) for c in range(48)]
            for (c0_, M_) in plan:
                WS.add(w_in[L_, :, c0_:c0_ + M_], 16, M_, key=("in", L_, c0_, M_))
            for n in range(NT):
                for b in range(3):
                    WS.add(wbs_all[b][L_, :, n * 128:(n + 1) * 128], 8, key=("br", L_, b, n))
            for n in range(NT):
                WS.add(w_out[L_, :, n * 128:(n + 1) * 128], 16, key=("out", L_, n))
            for f in range(NFT):
                WS.add(w_fg[L_, :, f * 128:(f + 1) * 128], 16, key=("fg", L_, f))
                WS.add(w_fu[L_, :, f * 128:(f + 1) * 128], 16, key=("fu", L_, f))
            for half in range(2):
                for n in range(NT):
                    for j, (k0, nk) in enumerate(((0, 16), (2048, 16), (4096, 12))):
                        WS.add(w_fd[L_, k0:k0 + nk * 128, n * 128:(n + 1) * 128], nk, key=("fd", L_, half, n, j))
            for n in range(NT):
                WS.add(w_pg[L_, :, n * 128:(n + 1) * 128], 16, key=("pg", L_, n))
                WS.add(w_pp[L_, :, n * 128:(n + 1) * 128], 2, key=("pp", L_, n))

        with ExitStack() as sp2:
            COS = sb(sp2, "COSp", [32, T], F32)
            SINS = sb(sp2, "SINSp", [32, T], F32)
            bcos, bsin = S.buf("COSp"), S.buf("SINSp")
            with ExitStack() as spr:
                posi = sb(spr, "posi", [32, T], I32)
                ta = sb(spr, "rta", [32, T], F32)
                tb = sb(spr, "rtb", [32, T], F32)
                ti = sb(spr, "rti", [32, T], I32)
                bpi, bta, btb, bti = S.buf("posi"), S.buf("rta"), S.buf("rtb"), S.buf("rti")
                S.dma([(posi[:], pos_in.broadcast_to([32, T]))], w=[bpi])
                S.op("dve", lambda: nc.vector.tensor_copy(out=ta[:], in_=posi[:]), r=[bpi], w=[bta])
                S.op("dve", lambda: nc.vector.tensor_scalar(
                    out=ta[:], in0=ta[:], scalar1=cst[0:32, C_INVF:C_INVF + 1], scalar2=1.0 / (2 * math.pi),
                    op0=ALU.mult, op1=ALU.mult), r=[bta, b_cst], w=[bta])
                for which, dst, bdst in (("sin", SINS, bsin), ("cos", COS, bcos)):
                    sh = 0.0 if which == "sin" else 0.25
                    S.op("dve", lambda sh=sh: nc.vector.tensor_scalar(
                        out=tb[:], in0=ta[:], scalar1=sh, scalar2=None, op0=ALU.add), r=[bta], w=[btb])
                    S.op("dve", lambda: nc.vector.tensor_copy(out=ti[:], in_=tb[:]), r=[btb], w=[bti])
                    S.op("dve", lambda dst=dst: nc.vector.tensor_copy(out=dst[:], in_=ti[:]),
                         r=[bti], w=[bdst])
                    S.op("dve", lambda dst=dst: nc.vector.tensor_tensor(
                        out=tb[:], in0=tb[:], in1=dst[:], op=ALU.subtract), r=[btb, bdst], w=[btb])
                    S.op("dve", lambda dst=dst: nc.vector.tensor_scalar(
                        out=dst[:], in0=tb[:], scalar1=0.5, scalar2=None, op0=ALU.is_gt),
                        r=[btb], w=[bdst])
                    S.op("dve", lambda dst=dst: nc.vector.tensor_tensor(
                        out=tb[:], in0=tb[:], in1=dst[:], op=ALU.subtract), r=[btb, bdst], w=[btb])
                    S.op("dve", lambda dst=dst: nc.vector.tensor_scalar(
                        out=dst[:], in0=tb[:], scalar1=-0.5, scalar2=None, op0=ALU.is_lt),
                        r=[btb], w=[bdst])
                    S.op("dve", lambda dst=dst: nc.vector.tensor_tensor(
                        out=tb[:], in0=tb[:], in1=dst[:], op=ALU.add), r=[btb, bdst], w=[btb])
                    S.op("act", lambda dst=dst: nc.scalar.activation(
                        out=dst[:], in_=tb[:], func=AF.Sin, scale=2 * math.pi * (1 - 2e-6)),
                        r=[btb], w=[bdst])
                S.op("dve", lambda: nc.vector.tensor_scalar(
                    out=SINS[0:16, :], in0=SINS[0:16, :], scalar1=-1.0, scalar2=None, op0=ALU.mult),
                    r=[bsin], w=[bsin])
                S.dma([(rotd[0], COS[:])], r=[bcos])
            S.dma([(rotd[1], SINS[:])], r=[bsin])
            S.barrier()

        x32_cur = xT_in

        for L in range(n_layers):
            with ExitStack() as ph:
                xbf, xb = load_xbf(ph, from32=(xT_in if L == 0 else None))
                rhs_x = lambda kg, t0: (xbf[:, kg, t0:t0 + 512], xb[kg])
                ws = WS
                pgb = [S.buf("pg0"), S.buf("pg1")]

                def wcol(c0, M=128):
                    return ws.take(("in", L, c0, M))

                with ExitStack() as sp1:
                    units = []
                    for c in range(8):
                        units.append((wcol(O_CV + c * 128), wcol(O_CG + c * 128)))
                    ytr = _Ring(S, nc, sp1, "yt", 3, [128, T], F32)
                    s1 = sb(sp1, "s1", [128, T], F32)
                    s2 = sb(sp1, "s2", [128, T], F32)
                    bs1, bs2 = S.buf("s1"), S.buf("s2")
                    sp1i = sp1.enter_context(ExitStack())
                    sgr = _Ring(S, nc, sp1i, "sg", 2, [128, T], F32)
                    ur = _Ring(S, nc, sp1i, "ubf", 2, [128, 30 + T], BF16)
                    dgr = _Ring(S, nc, sp1i, "dg", 2, [128, 31, 128], BF16)
                    sq = sb(sp1i, "sq", [128, T], F32)
                    bsq = S.buf("sq")
                    for i in range(2):
                        S.op("pool", lambda i=i: nc.gpsimd.memset(ur.t[i][:, 0:30], 0.0), w=[ur.b[i]])
                    S.op("pool", lambda: nc.gpsimd.memset(s1[:], 0.0), w=[bs1])
                    S.op("pool", lambda: nc.gpsimd.memset(s2[:], 0.0), w=[bs2])
                    for c in range(8):
                        uv, ug = units[c]
                        gemm_job(ws, [uv], [16], rhs_x, 128, PGA[0], pgb[0])
                        gemm_job(ws, [ug], [16], rhs_x, 128, PGA[1], pgb[1])
                        sg, bsg = sgr.next()
                        S.op("act", lambda sg=sg, c=c: nc.scalar.activation(
                            out=sg[:], in_=PGA[1], func=AF.Sigmoid, bias=cvc(L, "b_cg", c), scale=1.0),
                            r=[pgb[1], b_cv], w=[bsg])
                        u, bu = ur.next()
                        S.op("dve", lambda u=u, sg=sg, c=c: nc.vector.scalar_tensor_tensor(
                            out=u[:, 30:30 + T], in0=PGA[0], scalar=cvc(L, "b_cv", c), in1=sg[:],
                            op0=ALU.add, op1=ALU.mult), r=[pgb[0], bsg, b_cv], w=[bu])
                        dg, bdg = dgr.next()
                        S.op("dve", lambda dg=dg, c=c: nc.vector.tensor_tensor(
                            out=dg[:], in0=ident_bf.unsqueeze(1).to_broadcast([128, 31, 128]),
                            in1=cvc(L, "conv_w", c * 31, 31).unsqueeze(2).to_broadcast([128, 31, 128]),
                            op=ALU.mult), r=[b_cbf, b_cv], w=[bdg])
                        for j in range(31):
                            for ch in range(4):
                                S.op("pe", lambda dg=dg, u=u, j=j, ch=ch: nc.tensor.matmul(
                                    PGA[1][:, ch * 512:(ch + 1) * 512], lhsT=dg[:, j, :],
                                    rhs=u[:, ch * 512 + j: ch * 512 + j + 512], start=(j == 0), stop=(j == 30)),
                                    r=[bdg, bu], w=[pgb[1]], inc=(j == 30 and ch == 3))
                        yt, byt = ytr.next()
                        S.op("act", lambda yt=yt, c=c: nc.scalar.activation(
                            out=yt[:], in_=PGA[1], func=AF.Identity, bias=cvc(L, "conv_b", c), scale=1.0),
                            r=[pgb[1], b_cv], w=[byt])
                        S.op("act", lambda yt=yt: nc.scalar.activation(out=sq[:], in_=yt[:], func=AF.Square),
                             r=[byt], w=[bsq])
                        S.op("dve", lambda yt=yt: nc.vector.tensor_tensor(out=s1[:], in0=s1[:], in1=yt[:], op=ALU.add),
                             r=[byt, bs1], w=[bs1])
                        S.op("dve", lambda: nc.vector.tensor_tensor(out=s2[:], in0=s2[:], in1=sq[:], op=ALU.add),
                             r=[bsq, bs2], w=[bs2])
                        S.dma([(r32d[c * 128:(c + 1) * 128, :], yt[:])], r=[byt])
                    S.barrier()
                    sp1i.close()
                    rstd, nmr, brs, bnm = ln_finish(sp1, s1, s2, bs1, bs2, 1024.0)
                    ucr = _Ring(S, nc, sp1, "ucbf", 2, [128, T], BF16)
                    for c in range(8):
                        yt, byt = ytr.next()
                        S.dma([(yt[:], r32d[c * 128:(c + 1) * 128, :])], w=[byt])
                        S.op("dve", lambda yt=yt: nc.vector.tensor_tensor(out=yt[:], in0=yt[:], in1=rstd[:],
                                                                          op=ALU.mult), r=[byt, brs], w=[byt])
                        S.op("dve", lambda yt=yt: nc.vector.tensor_tensor(out=yt[:], in0=yt[:], in1=nmr[:],
                                                                          op=ALU.add), r=[byt, bnm], w=[byt])
                        uc, buc = ucr.next()
                        S.op("act", lambda yt=yt, uc=uc, c=c: nc.scalar.activation(
                            out=uc[:], in_=yt[:], func=AF.Silu, bias=cvc(L, "cln_b", c), scale=cvc(L, "cln_g", c)),
                            r=[byt, b_cv], w=[buc])
                        S.dma([(brT[c * 128:(c + 1) * 128, :], uc[:])], r=[buc])
                    S.barrier()
                if stop == "A1":
                    break
                with ExitStack() as sp2:
                    COS = sb(sp2, "COS", [32, T], F32)
                    SINS = sb(sp2, "SINS", [32, T], F32)
                    bcos, bsin = S.buf("COS"), S.buf("SINS")
                    kmean = sb(sp2, "kmean", [128, 64], F32)
                    bkm = S.buf("kmean")
                    S.dma([(COS[:], rotd[0])], w=[bcos])
                    S.dma([(SINS[:], rotd[1])], w=[bsin])
                    t32r = _Ring(S, nc, sp2, "t32", 2, [128, T], F32)
                    swr = _Ring(S, nc, sp2, "sw", 2, [32, T], F32)
                    bfr = _Ring(S, nc, sp2, "a2bf", 3, [128, T], BF16)
                    jn = 0
                    for h in range(8):
                        uk = wcol(O_K + h * 128)
                        uvv = wcol(O_V + h * 128)
                        uq = wcol(O_Q + h * 128)
                        for kind, u in (("k", uk), ("v", uvv), ("q", uq)):
                            g = jn % 2
                            jn += 1
                            gemm_job(ws, [u], [16], rhs_x, 128, PGA[g], pgb[g])
                            o, ob = bfr.next()
                            if kind == "v":
                                S.op("act", lambda o=o, g=g, h=h: nc.scalar.activation(
                                    out=o[:], in_=PGA[g], func=AF.Identity, bias=cvc(L, "b_v", h), scale=1.0),
                                    r=[pgb[g], b_cv], w=[ob])
                                S.dma([(vTd[h * 128:(h + 1) * 128, :], o[:])], r=[ob])
                                continue
                            t, bt = t32r.next()
                            S.op("act", lambda t=t, g=g, h=h, kind=kind: nc.scalar.activation(
                                out=t[:], in_=PGA[g], func=AF.Identity, bias=cvc(L, "b_" + kind, h), scale=1.0),
                                r=[pgb[g], b_cv], w=[bt])
                            sw, bsw = swr.next()
                            S.dma([(sw[0:16, :], t[16:32, :]), (sw[16:32, :], t[0:16, :])], r=[bt], w=[bsw])
                            S.op("dve", lambda t=t: nc.vector.tensor_tensor(
                                out=t[0:32, :], in0=t[0:32, :], in1=COS[:], op=ALU.mult), r=[bt, bcos], w=[bt])
                            S.op("dve", lambda sw=sw: nc.vector.tensor_tensor(
                                out=sw[:], in0=sw[:], in1=SINS[:], op=ALU.mult), r=[bsw, bsin], w=[bsw])
                            S.op("dve", lambda t=t, sw=sw: nc.vector.tensor_tensor(
                                out=t[0:32, :], in0=t[0:32, :], in1=sw[:], op=ALU.add), r=[bt, bsw], w=[bt])
                            S.op("dve", lambda t=t, o=o: nc.vector.tensor_copy(out=o[:], in_=t[:]), r=[bt], w=[ob])
                            if kind == "k":
                                S.op("dve", lambda t=t, h=h: nc.vector.tensor_reduce(
                                    out=kmean[:, h * 8:(h + 1) * 8], in_=t[:].rearrange("p (b k) -> p b k", k=256),
                                    axis=AX.X, op=ALU.add), r=[bt], w=[bkm])
                                S.dma([(kTd[h * 128:(h + 1) * 128, :], o[:])], r=[ob])
                            else:
                                S.dma([(qTd[h * 128:(h + 1) * 128, :], o[:])], r=[ob])
                                S.dma([(q32d[h * 128:(h + 1) * 128, :], t[:, 1024:2048])], r=[bt])
                    S.dma([(kmd[:, :], kmean[:])], r=[bkm])
                    S.barrier()
                if stop == "A2":
                    break
                with ExitStack() as sp3:
                    prer = _Ring(S, nc, sp3, "mpre", 2, [128, 3 + T], F32)
                    accr = _Ring(S, nc, sp3, "macc", 2, [128, T], F32)
                    bfr = _Ring(S, nc, sp3, "a3bf", 3, [128, T], BF16)
                    for i in range(2):
                        S.op("pool", lambda i=i: nc.gpsimd.memset(prer.t[i][:, 0:3], 0.0), w=[prer.b[i]])
                    jn = 0
                    for c in range(16):
                        u = wcol(O_QKM + c * 128)
                        g = jn % 2
                        jn += 1
                        gemm_job(ws, [u], [16], rhs_x, 128, PGA[g], pgb[g])
                        pre, bpre = prer.next()
                        S.op("act", lambda pre=pre, g=g, c=c: nc.scalar.activation(
                            out=pre[:, 3:3 + T], in_=PGA[g], func=AF.Identity, bias=cvc(L, "b_qkm", c), scale=1.0),
                            r=[pgb[g], b_cv], w=[bpre])
                        acc, bacc = accr.next()
                        S.op("dve", lambda pre=pre, acc=acc, c=c: nc.vector.tensor_scalar(
                            out=acc[:], in0=pre[:, 0:T], scalar1=cvc(L, "mconv_w", c * 4), scalar2=cvc(L, "mconv_b", c),
                            op0=ALU.mult, op1=ALU.add), r=[bpre, b_cv], w=[bacc])
                        for j in range(1, 4):
                            S.op("dve", lambda pre=pre, acc=acc, c=c, j=j: nc.vector.scalar_tensor_tensor(
                                out=acc[:], in0=pre[:, j:j + T], scalar=cvc(L, "mconv_w", c * 4 + j), in1=acc[:],
                                op0=ALU.mult, op1=ALU.add), r=[bpre, bacc, b_cv], w=[bacc])
                        o, ob = bfr.next()
                        S.op("act", lambda o=o, acc=acc: nc.scalar.activation(out=o[:], in_=acc[:], func=AF.Silu),
                             r=[bacc], w=[ob])
                        S.dma([(qkmd[c * 128:(c + 1) * 128, :], o[:])], r=[ob])
                    for kind, off, bname, fn, dst in (("vm", O_VM, "b_vm", AF.Identity, vmd),
                                                     ("om", O_OM, "b_om", AF.Sigmoid, sgod)):
                        for c in range(8):
                            u = wcol(off + c * 128)
                            g = jn % 2
                            jn += 1
                            gemm_job(ws, [u], [16], rhs_x, 128, PGA[g], pgb[g])
                            o, ob = bfr.next()
                            S.op("act", lambda o=o, g=g, c=c, bname=bname, fn=fn: nc.scalar.activation(
                                out=o[:], in_=PGA[g], func=fn, bias=cvc(L, bname, c), scale=1.0),
                                r=[pgb[g], b_cv], w=[ob])
                            S.dma([(dst[c * 128:(c + 1) * 128, :], o[:])], r=[ob])
                    ig = sb(sp3, "g_ig", [4, T], F32)
                    lfm = sb(sp3, "g_lfm", [4, T], F32)
                    csm = sb(sp3, "g_cs", [4, T], F32)
                    ga = sb(sp3, "g_a", [4, T], F32)
                    gA = sb(sp3, "g_A", [4, T], F32)
                    gone = sb(sp3, "g_one", [4, T], F32)
                    nbf = sb(sp3, "g_nbf", [4, 1], F32)
                    big, blfm, bcs, bga, bgA, bone, bnbf = [S.buf("g") for _ in range(7)]
                    S.op("pool", lambda: nc.gpsimd.memset(gone[:], 1.0), w=[bone])
                    S.op("dve", lambda: nc.vector.tensor_scalar(out=nbf[:], in0=cvc(L, "b_f")[0:4, :], scalar1=-1.0,
                                                                scalar2=None, op0=ALU.mult), r=[b_cv], w=[bnbf])
                    ui = wcol(O_I, 4)
                    uf = wcol(O_F, 4)
                    gi = jn % 2
                    jn += 1
                    gemm_job(ws, [ui], [16], rhs_x, 4, PGA[gi], pgb[gi])
                    gf = jn % 2
                    jn += 1
                    gemm_job(ws, [uf], [16], rhs_x, 4, PGA[gf], pgb[gf])
                    S.op("act", lambda: nc.scalar.activation(out=ig[:], in_=PGA[gi][0:4, :], func=AF.Identity,
                                                             bias=cvc(L, "b_i")[0:4, :], scale=1.0),
                         r=[pgb[gi], b_cv], w=[big])
                    S.op("act", lambda: nc.scalar.activation(out=lfm[:], in_=PGA[gf][0:4, :], func=AF.Exp,
                                                             bias=nbf[:], scale=-1.0), r=[pgb[gf], bnbf], w=[blfm])
                    S.op("act", lambda: nc.scalar.activation(out=lfm[:], in_=lfm[:], func=AF.Ln,
                                                             bias=ones32[0:4, 0:1], scale=1.0),
                         r=[blfm, b_cst], w=[blfm])
                    S.op("dve", lambda: nc.vector.tensor_tensor_scan(out=csm[:], data0=gone[:], data1=lfm[:],
                                                                     initial=0.0, op0=ALU.mult, op1=ALU.add),
                         r=[bone, blfm], w=[bcs])
                    S.op("dve", lambda: nc.vector.tensor_tensor(out=ga[:], in0=ig[:], in1=csm[:], op=ALU.add),
                         r=[big, bcs], w=[bga])
                    S.op("dve", lambda: nc.vector.tensor_tensor_scan(out=gA[:], data0=ga[:], data1=ga[:],
                                                                     initial=0.0, op0=ALU.max, op1=ALU.max),
                         r=[bga], w=[bgA])
                    S.op("dve", lambda: nc.vector.tensor_tensor(out=csm[:], in0=csm[:], in1=gA[:], op=ALU.subtract),
                         r=[bcs, bgA], w=[bcs])
                    S.op("act", lambda: nc.scalar.activation(out=csm[:], in_=csm[:], func=AF.Exp), r=[bcs], w=[bcs])
                    S.op("dve", lambda: nc.vector.tensor_scalar(out=gA[:], in0=gA[:], scalar1=-1.0, scalar2=None,
                                                                op0=ALU.mult), r=[bgA], w=[bgA])
                    S.dma([(gated[0], ga[:])], r=[bga])
                    S.dma([(gated[1], gA[:])], r=[bgA])
                    S.dma([(gated[2], csm[:])], r=[bcs])
                    S.barrier()
                if stop == "A3":
                    break
                with ExitStack() as sp4:
                    bfr = _Ring(S, nc, sp4, "a4bf", 3, [128, T], BF16)
                    for c in range(48):
                        u = wcol(O_G + c * 128)
                        g = c % 2
                        gemm_job(ws, [u], [16], rhs_x, 128, PGA[g], pgb[g])
                        o, ob = bfr.next()
                        S.op("act", lambda o=o, g=g, c=c: nc.scalar.activation(
                            out=o[:], in_=PGA[g], func=AF.Sigmoid, bias=cvc(L, "b_g", c), scale=1.0),
                            r=[pgb[g], b_cv], w=[ob])
                        S.dma([(sgated[c * 128:(c + 1) * 128, :], o[:])], r=[ob])
                    S.barrier()
            if stop in ("A1", "A2", "A3", "A4"):
                break
            with ExitStack() as ph:
                qr = _Ring(S, nc, ph, "qT", 2, [128, T], BF16)
                kr = _Ring(S, nc, ph, "kT", 2, [128, T], BF16)
                vr = _Ring(S, nc, ph, "vT", 2, [128, T], BF16)
                q32r = _Ring(S, nc, ph, "q32g", 2, [128, 1024], F32)
                Vr = _Ring(S, nc, ph, "Vtok", 2, [128, 16, 128], BF16)
                btr = _Ring(S, nc, ph, "biasT", 2, [8, 1024], BF16)
                otr = _Ring(S, nc, ph, "oT", 2, [128, T], BF16)
                ptr = _Ring(S, nc, ph, "PT", 3, [128, 512], BF16)
                rir = _Ring(S, nc, ph, "rinv", 2, [128, 512], F32)
                gsm = [sb(ph, "gs%d" % i, [128, 64], F32) for i in range(4)]
                bgs = [S.buf("gs") for _ in range(4)]
                mm = sb(ph, "gmax", [128, 8], F32)
                bmm = S.buf("gmax")
                kmean = sb(ph, "kmeanB", [128, 64], F32)
                bkm = S.buf("kmeanB")
                S.dma([(kmean[:], kmd[:, :])], w=[bkm])
                pS = [S.buf("pS0"), S.buf("pS1")]
                pO = [S.buf("pO0"), S.buf("pO1")]
                pR = [S.buf("pR0"), S.buf("pR1")]
                pTPb = S.buf("pTP")
                pTP = [pTPb, pTPb]
                pG = S.buf("pG7")
                pBT = pG
                SCL = 128.0 ** -0.5
                heads = {}

                def prologue1(h):
                    q, bq_ = qr.next()
                    k, bk_ = kr.next()
                    v, bv_ = vr.next()
                    q32, bq32 = q32r.next()
                    S.dma([(q[:], qTd[h * 128:(h + 1) * 128, :])], w=[bq_])
                    S.dma([(k[:], kTd[h * 128:(h + 1) * 128, :])], w=[bk_])
                    S.dma([(v[:], vTd[h * 128:(h + 1) * 128, :])], w=[bv_])
                    S.dma([(q32[:], q32d[h * 128:(h + 1) * 128, :])], w=[bq32])
                    V, bV = Vr.next()
                    for half in range(2):
                        tpv = ps[:, 6, :].bitcast(BF16)
                        for i in range(8):
                            tt = half * 8 + i
                            S.op("pe", lambda tpv=tpv, i=i, tt=tt, v=v: nc.tensor.transpose(
                                out=tpv[:, i * 128:(i + 1) * 128], in_=v[:, tt * 128:(tt + 1) * 128],
                                identity=ident_bf), r=[bv_, b_cbf], w=[pTP[half]], inc=(i == 7))
                        S.op("act", lambda tpv=tpv, half=half, V=V: nc.scalar.copy(
                            out=V[:, half * 8:(half + 1) * 8, :],
                            in_=tpv[:, :].rearrange("p (a b) -> p a b", b=128)), r=[pTP[half]], w=[bV])
                    for qt in range(8):
                        S.op("pe", lambda qt=qt, q32=q32, h=h: nc.tensor.matmul(
                            ps[:, 7, qt * 8:(qt + 1) * 8], lhsT=q32[:, qt * 128:(qt + 1) * 128],
                            rhs=kmean[:, h * 8:(h + 1) * 8], start=True, stop=True),
                            r=[bq32, bkm], w=[pG], inc=(qt == 7))
                    g, e, g2, bq = gsm
                    bg, be, bg2, bbq = bgs
                    g3 = lambda t: t[:].rearrange("p (a b) -> p a b", b=8)
                    mb = mm[:].unsqueeze(2).to_broadcast([128, 8, 8])
                    S.op("dve", lambda: nc.vector.tensor_tensor(out=g[:], in0=ps[:, 7, 0:64],
                                                                in1=cst[:, C_GMASK:C_GMASK + 64], op=ALU.add),
                         r=[pG, b_cst], w=[bg])
                    src, bsrc = g, bg
                    for it in range(2):
                        S.op("dve", lambda src=src: nc.vector.tensor_reduce(out=mm[:], in_=g3(src), axis=AX.X,
                                                                            op=ALU.max), r=[bsrc], w=[bmm])
                        S.op("dve", lambda src=src: nc.vector.tensor_tensor(out=g3(e), in0=g3(src), in1=mb,
                                                                            op=ALU.is_ge), r=[bsrc, bmm], w=[be])
                        S.op("dve", lambda src=src: nc.vector.scalar_tensor_tensor(
                            out=g2[:], in0=e[:], scalar=-1e30, in1=src[:], op0=ALU.mult, op1=ALU.add),
                            r=[be, bsrc], w=[bg2])
                        src, bsrc = g2, bg2
                    S.op("dve", lambda: nc.vector.tensor_reduce(out=mm[:], in_=g3(g2), axis=AX.X, op=ALU.max),
                         r=[bg2], w=[bmm])
                    S.op("dve", lambda: nc.vector.tensor_tensor(out=g3(e), in0=g3(g), in1=mb, op=ALU.is_lt),
                         r=[bg, bmm], w=[be])
                    S.op("dve", lambda: nc.vector.tensor_scalar(out=bq[:], in0=e[:], scalar1=NEG, scalar2=None,
                                                                op0=ALU.mult), r=[be], w=[bbq])
                    heads[h] = (q, bq_, k, bk_, V, bV)

                def prologue2(h):
                    bq, bbq = gsm[3], bgs[3]
                    bT, bbT = btr.next()
                    for half in range(2):
                        for i in range(4):
                            qt = half * 4 + i
                            S.op("pe", lambda i=i, qt=qt: nc.tensor.transpose(
                                out=ps[0:8, 7, i * 128:(i + 1) * 128], in_=bq[:, qt * 8:(qt + 1) * 8],
                                identity=ident32), r=[bbq, b_cst], w=[pBT], inc=(i == 3))
                        S.op("act", lambda half=half, bT=bT: nc.scalar.copy(
                            out=bT[:, half * 512:(half + 1) * 512], in_=ps[0:8, 7, :]), r=[pBT], w=[bbT])
                    heads[h] = heads[h] + (bT, bbT)

                def core(h):
                    q, bq_, k, bk_, V, bV, bT, bbT = heads.pop(h)
                    oT, boT = otr.next()
                    steps = []
                    for qg in range(4):
                        nkt = 4 * qg + 4
                        for kt in range(nkt):
                            steps.append((qg, kt, nkt))
                    st = {}

                    def stage1(i):
                        qg, kt, nkt = steps[i]
                        kb = kt // 2
                        segs = []
                        for s_ in range(4):
                            qsub = 4 * qg + s_
                            qblk = qsub // 2
                            if kb > qblk or (kb == qblk and kt > qsub):
                                segs.append(None)
                            elif kb == qblk:
                                segs.append(("diag" if kt == qsub else "full", False))
                            else:
                                segs.append(("full", qblk >= 4))
                        c0 = min(j for j in range(4) if segs[j] is not None) * 128
                        cb = [j for j in range(4) if segs[j] is not None and segs[j][1]]
                        cd = [j for j in range(4) if segs[j] is not None and segs[j][0] == "diag"]
                        g = i % 2
                        Sb = ps[:, g, :]
                        nextra = (1 if cb else 0) + len(cd)
                        S.op("pe", lambda: nc.tensor.matmul(
                            Sb[:, c0:512], lhsT=k[:, kt * 128:(kt + 1) * 128],
                            rhs=q[:, qg * 512 + c0:qg * 512 + 512], start=True, stop=(nextra == 0)),
                            r=[bk_, bq_], w=[pS[g]], inc=(nextra == 0))
                        if cb:
                            b0 = min(cb) * 128
                            nextra -= 1
                            S.op("pe", lambda nextra=nextra: nc.tensor.matmul(
                                Sb[:, b0:512], lhsT=cbf[0:8, 384 + kb * 128:384 + (kb + 1) * 128],
                                rhs=bT[0:8, qg * 512 + b0 - 1024:qg * 512 + 512 - 1024],
                                start=False, stop=(nextra == 0)),
                                r=[bbT, b_cbf], w=[pS[g]], inc=(nextra == 0))
                        for j in cd:
                            nextra -= 1
                            S.op("pe", lambda j=j, nextra=nextra: nc.tensor.matmul(
                                Sb[:, j * 128:(j + 1) * 128], lhsT=ident_bf, rhs=tri_bf,
                                start=False, stop=(nextra == 0)),
                                r=[b_cbf], w=[pS[g]], inc=(nextra == 0))
                        PT, bPT = ptr.next()
                        S.op("act", lambda: nc.scalar.activation(
                            out=PT[:, c0:512], in_=Sb[:, c0:512], func=AF.Exp, scale=SCL),
                            r=[pS[g]], w=[bPT])
                        st[i] = (PT, bPT, c0)

                    def stage2(i):
                        qg, kt, nkt = steps[i]
                        PT, bPT, c0 = st.pop(i)
                        a = qg % 2
                        S.op("pe", lambda: nc.tensor.matmul(
                            ps[:, 2 + 2 * a, c0:512], lhsT=V[:, kt, :], rhs=PT[:, c0:512],
                            start=(kt == 0), stop=(kt == nkt - 1)), r=[bV, bPT], w=[pO[a]], inc=(kt == nkt - 1))
                        S.op("pe", lambda: nc.tensor.matmul(
                            ps[:, 3 + 2 * a, c0:512], lhsT=ones_bf, rhs=PT[:, c0:512],
                            start=(kt == 0), stop=(kt == nkt - 1)), r=[b_cbf, bPT], w=[pR[a]], inc=(kt == nkt - 1))
                        if kt == nkt - 1:
                            ri, bri = rir.next()
                            S.op("dve", lambda: nc.vector.reciprocal(out=ri[:], in_=ps[:, 3 + 2 * a, :]),
                                 r=[pR[a]], w=[bri])
                            S.op("dve", lambda: nc.vector.tensor_tensor(
                                out=oT[:, qg * 512:(qg + 1) * 512], in0=ps[:, 2 + 2 * a, :], in1=ri[:], op=ALU.mult),
                                r=[pO[a], bri], w=[boT])

                    stage1(0)
                    for i in range(len(steps)):
                        if i + 1 < len(steps):
                            stage1(i + 1)
                        stage2(i)
                    S.dma([(brT[1024 + h * 128:1024 + (h + 1) * 128, :], oT[:])], r=[boT])

                prologue1(0)
                prologue2(0)
                for h in range(8):
                    if h + 1 < 8:
                        prologue1(h + 1)
                    core(h)
                    if h + 1 < 8:
                        prologue2(h + 1)
                S.barrier()
            if stop == "B2":
                break
            with ExitStack() as ph:
                a4 = sb(ph, "m_a4", [4, T], F32)
                negA = sb(ph, "m_negA", [4, T], F32)
                em = sb(ph, "m_em", [4, T], F32)
                acol = sb(ph, "m_acol", [128, 64], F32)
                ba4, bnA, bem, bacol = [S.buf("mg") for _ in range(4)]
                S.dma([(a4[:], gated[0])], w=[ba4])
                S.dma([(negA[:], gated[1])], w=[bnA])
                S.dma([(em[:], gated[2])], w=[bem])
                nA2 = sb(ph, "m_nA2", [36, T], BF16)
                hs36 = sb(ph, "m_hs36", [36, 512], BF16)
                em2 = sb(ph, "m_em2", [36, T], BF16)
                bnA2, bhs, bem2 = S.buf("m_nA2"), S.buf("m_hs36"), S.buf("m_em2")
                with ExitStack() as tmps:
                    hi32 = sb(tmps, "m_hi32", [4, T], F32)
                    lo16 = sb(tmps, "m_lo16", [4, T], BF16)
                    bhi, blo = S.buf("m_hi32"), S.buf("m_lo16")
                    S.op("pool", lambda: nc.gpsimd.memset(nA2[:], 0.0), w=[bnA2])
                    S.op("pool", lambda: nc.gpsimd.memset(hs36[:], 0.0), w=[bhs])
                    S.op("dve", lambda: nc.vector.tensor_copy(out=nA2[0:4, :], in_=negA[:]), r=[bnA], w=[bnA2])
                    S.op("dve", lambda: nc.vector.tensor_copy(out=hi32[:], in_=nA2[0:4, :]), r=[bnA2], w=[bhi])
                    S.op("dve", lambda: nc.vector.tensor_tensor(out=hi32[:], in0=negA[:], in1=hi32[:],
                                                                op=ALU.subtract), r=[bnA, bhi], w=[bhi])
                    S.op("dve", lambda: nc.vector.tensor_copy(out=lo16[:], in_=hi32[:]), r=[bhi], w=[blo])
                    S.dma([(nA2[32:36, :], lo16[:])], r=[blo], w=[bnA2])
                    S.op("dve", lambda: nc.vector.tensor_copy(out=hs36[0:4, :], in_=cst[0:4, C_HSEL:C_HSEL + 512]),
                         r=[b_cst], w=[bhs])
                    S.dma([(hs36[32:36, :], hs36[0:4, :])], r=[bhs], w=[bhs])
                    S.op("pool", lambda: nc.gpsimd.memset(em2[:], 0.0), w=[bem2])
                    S.op("dve", lambda: nc.vector.tensor_copy(out=em2[0:4, :], in_=em[:]), r=[bem], w=[bem2])
                    S.op("dve", lambda: nc.vector.tensor_copy(out=hi32[:], in_=em2[0:4, :]), r=[bem2, blo], w=[bhi])
                    S.op("dve", lambda: nc.vector.tensor_tensor(out=hi32[:], in0=em[:], in1=hi32[:],
                                                                op=ALU.subtract), r=[bem, bhi], w=[bhi])
                    S.op("dve", lambda: nc.vector.tensor_copy(out=lo16[:], in_=hi32[:]), r=[bhi], w=[blo])
                    S.dma([(em2[32:36, :], lo16[:])], r=[blo], w=[bem2])
                    S.barrier()
                qmr = _Ring(S, nc, ph, "qm", 2, [128, 2, T], BF16)
                kmr = _Ring(S, nc, ph, "km", 2, [128, 2, T], BF16)
                vmr = _Ring(S, nc, ph, "vm", 2, [128, 2, T], BF16)
                sgr = _Ring(S, nc, ph, "sgo", 2, [128, 2, T], BF16)
                Vr = _Ring(S, nc, ph, "Vm", 2, [128, 16, 256], BF16)
                hor = _Ring(S, nc, ph, "hout", 1, [128, 2, T], BF16)
                dtr = _Ring(S, nc, ph, "DT", 2, [128, 512], F32)
                ptr = _Ring(S, nc, ph, "PTm", 3, [128, 512], BF16)
                evr = _Ring(S, nc, ph, "mev", 2, [128, 3, 512], F32)
                cnr = _Ring(S, nc, ph, "mcn", 2, [128, 2], F32)
                eqr = _Ring(S, nc, ph, "meq", 2, [128, 512], F32)
                ekr = _Ring(S, nc, ph, "mek", 2, [128, 16], F32)
                tmp = [sb(ph, "mtmp%d" % i, [128, 512], F32) for i in range(3)]
                btmp = [S.buf("mtmp") for _ in range(3)]
                pS = [S.buf("pS0"), S.buf("pS1")]
                pD = [S.buf("pD0"), S.buf("pD1")]
                pN = [S.buf("pN0"), S.buf("pN1"), S.buf("pDen")]
                pT7 = S.buf("pT7")
                for tt in range(16):
                    S.op("pe", lambda tt=tt: nc.tensor.transpose(
                        out=ps[:, 7, tt * 4:(tt + 1) * 4], in_=a4[0:4, tt * 128:(tt + 1) * 128],
                        identity=ident32[0:4, 0:4]), r=[ba4, b_cst], w=[pT7], inc=(tt == 15))
                S.op("act", lambda: nc.scalar.copy(out=acol[:], in_=ps[:, 7, 0:64]), r=[pT7], w=[bacol])
                mh = {}

                def mprologue(h):
                    qm, bqm = qmr.next()
                    km, bkm_ = kmr.next()
                    vm, bvm = vmr.next()
                    sg, bsg = sgr.next()
                    v3 = lambda ap_: ap_.rearrange("(c p) t -> p c t", p=128)
                    S.dma([(qm[:], v3(qkmd[2 * h * 128:(2 * h + 2) * 128, :]))], w=[bqm])
                    S.dma([(km[:], v3(qkmd[(8 + 2 * h) * 128:(10 + 2 * h) * 128, :]))], w=[bkm_])
                    S.dma([(vm[:], v3(vmd[2 * h * 128:(2 * h + 2) * 128, :]))], w=[bvm])
                    S.dma([(sg[:], v3(sgod[2 * h * 128:(2 * h + 2) * 128, :]))], w=[bsg])
                    V, bV = Vr.next()
                    tpv = ps[:, 7, :].bitcast(BF16)
                    for c in range(2):
                        for half in range(2):
                            for i in range(8):
                                tt = half * 8 + i
                                S.op("pe", lambda i=i, tt=tt, c=c, vm=vm: nc.tensor.transpose(
                                    out=tpv[:, i * 128:(i + 1) * 128], in_=vm[:, c, tt * 128:(tt + 1) * 128],
                                    identity=ident_bf), r=[bvm, b_cbf], w=[pT7], inc=(i == 7))
                            S.op("act", lambda half=half, c=c, V=V: nc.scalar.copy(
                                out=V[:, half * 8:(half + 1) * 8, c * 128:(c + 1) * 128],
                                in_=tpv[:, :].rearrange("p (a b) -> p a b", b=128)), r=[pT7], w=[bV])
                    mh[h] = (qm, bqm, km, bkm_, V, bV, sg, bsg)

                def mcore(h):
                    qm, bqm, km, bkm_, V, bV, sg, bsg = mh.pop(h)
                    ho, bho = hor.next()
                    hsel = cst[0:4, C_HSEL + h * 128:C_HSEL + (h + 1) * 128]
                    steps = []
                    for qg in range(4):
                        nkt = 4 * qg + 4
                        for kt in range(nkt):
                            steps.append((qg, kt, nkt))
                    st = {}
                    cnt = [0]

                    qst = {}

                    def qg_setup(qg):
                        q0 = qg * 512
                        g = cnt[0] % 2
                        cnt[0] += 1
                        Db = ps[:, 2 + g, :]
                        S.op("pe", lambda: nc.tensor.matmul(
                            Db[:, 0:1], lhsT=hs36[0:36, h * 128:(h + 1) * 128], rhs=nA2[0:36, q0 - 1:q0],
                            start=True, stop=True), r=[bnA2, bhs], w=[pD[g]], inc=True)
                        cn, bcn = cnr.next()
                        S.op("dve", lambda: nc.vector.tensor_scalar(
                            out=cn[:, 0:1], in0=Db[:, 0:1], scalar1=math.log(0.0625), scalar2=None, op0=ALU.add),
                            r=[pD[g]], w=[bcn])
                        S.op("dve", lambda: nc.vector.tensor_scalar(
                            out=cn[:, 1:2], in0=Db[:, 0:1], scalar1=-1.0, scalar2=None, op0=ALU.mult),
                            r=[pD[g]], w=[bcn])
                        g2 = cnt[0] % 2
                        cnt[0] += 1
                        Db2 = ps[:, 2 + g2, :]
                        S.op("pe", lambda: nc.tensor.matmul(
                            Db2[:, :], lhsT=hs36[0:36, h * 128:(h + 1) * 128], rhs=nA2[0:36, q0:q0 + 512],
                            start=True, stop=True), r=[bnA2, bhs], w=[pD[g2]], inc=True)
                        eq, beq = eqr.next()
                        S.op("act", lambda: nc.scalar.activation(
                            out=eq[:], in_=Db2[:, :], func=AF.Exp, bias=cn[:, 1:2], scale=1.0),
                            r=[pD[g2], bcn], w=[beq])
                        ek, bek = ekr.next()
                        nk = 4 * qg
                        acol3 = acol[:].rearrange("p (k hh) -> p k hh", hh=4)
                        S.op("act", lambda: nc.scalar.activation(
                            out=ek[:, 0:nk], in_=acol3[:, 0:nk, h], func=AF.Exp, bias=cn[:, 0:1], scale=1.0),
                            r=[bacol, bcn], w=[bek])
                        qst[qg] = (eq, beq, ek, bek)

                    def stage1(i):
                        qg, kt, nkt = steps[i]
                        q0 = qg * 512
                        rr = kt - 4 * qg
                        c0 = max(0, rr) * 128
                        if kt == 0 and qg >= 1:
                            qg_setup(qg)
                        g = cnt[0] % 2
                        cnt[0] += 1
                        Sb = ps[:, g, :]
                        Db = ps[:, 2 + g, :]
                        for c in range(2):
                            S.op("pe", lambda c=c: nc.tensor.matmul(
                                Sb[:, c0:512], lhsT=km[:, c, kt * 128:(kt + 1) * 128],
                                rhs=qm[:, c, q0 + c0:q0 + 512], start=(c == 0), stop=(c == 1)),
                                r=[bkm_, bqm], w=[pS[g]], inc=(c == 1))
                        if rr < 0:
                            eq, beq, ek, bek = qst[qg]
                            PT, bPT = ptr.next()
                            S.op("dve", lambda: nc.vector.scalar_tensor_tensor(
                                out=PT[:, :], in0=Sb[:, :], scalar=ek[:, kt:kt + 1], in1=eq[:, :],
                                op0=ALU.mult, op1=ALU.mult), r=[pS[g], bek, beq], w=[bPT])
                            st[i] = (PT, bPT, 0)
                            return
                        S.op("pe", lambda: nc.tensor.matmul(
                            Db[:, c0:512], lhsT=hs36[0:36, h * 128:(h + 1) * 128],
                            rhs=nA2[0:36, q0 + c0:q0 + 512],
                            start=True, stop=(rr < 0)), r=[bnA2, bhs], w=[pD[g]], inc=(rr < 0))
                        if rr >= 0:
                            S.op("pe", lambda: nc.tensor.matmul(
                                Db[:, c0:c0 + 128], lhsT=ident_bf, rhs=tri_bf, start=False, stop=True),
                                r=[b_cbf], w=[pD[g]], inc=True)
                        DT, bDT = dtr.next()
                        S.op("act", lambda: nc.scalar.activation(
                            out=DT[:, c0:512], in_=Db[:, c0:512], func=AF.Exp,
                            bias=acol[:, kt * 4 + h:kt * 4 + h + 1], scale=1.0), r=[pD[g], bacol], w=[bDT])
                        PT, bPT = ptr.next()
                        S.op("dve", lambda: nc.vector.scalar_tensor_tensor(
                            out=PT[:, c0:512], in0=Sb[:, c0:512], scalar=0.0625, in1=DT[:, c0:512],
                            op0=ALU.mult, op1=ALU.mult), r=[pS[g], bDT], w=[bPT])
                        st[i] = (PT, bPT, c0)

                    def stage2(i):
                        qg, kt, nkt = steps[i]
                        q0 = qg * 512
                        PT, bPT, c0 = st.pop(i)
                        for c in range(3):
                            lw = V[:, kt, c * 128:(c + 1) * 128] if c < 2 else ones_bf
                            S.op("pe", lambda lw=lw, c=c: nc.tensor.matmul(
                                ps[:, 4 + c, c0:512], lhsT=lw, rhs=PT[:, c0:512],
                                start=(kt == 0), stop=(kt == nkt - 1)),
                                r=[bV, bPT, b_cbf], w=[pN[c]], inc=(kt == nkt - 1))
                        if kt != nkt - 1:
                            return
                        ev, bev = evr.next()
                        for c in range(3):
                            S.op("act", lambda c=c: nc.scalar.activation(
                                out=ev[:, c, :], in_=ps[:, 4 + c, :], func=(AF.Abs if c == 2 else AF.Copy)),
                                r=[pN[c]], w=[bev])
                        g = cnt[0] % 2
                        cnt[0] += 1
                        Db = ps[:, 2 + g, :]
                        S.op("pe", lambda: nc.tensor.matmul(
                            Db[:, :], lhsT=hs36[0:36, h * 128:(h + 1) * 128], rhs=em2[0:36, q0:q0 + 512],
                            start=True, stop=True), r=[bem2, bhs], w=[pD[g]], inc=True)
                        dn, rd, hh = tmp
                        bdn, brd, bhh = btmp
                        S.op("dve", lambda: nc.vector.tensor_tensor(
                            out=dn[:], in0=ev[:, 2, :], in1=Db[:, :], op=ALU.max), r=[bev, pD[g]], w=[bdn])
                        S.op("dve", lambda: nc.vector.reciprocal(out=rd[:], in_=dn[:]), r=[bdn], w=[brd])
                        for c in range(2):
                            S.op("dve", lambda c=c: nc.vector.tensor_tensor(
                                out=hh[:], in0=ev[:, c, :], in1=rd[:], op=ALU.mult), r=[bev, brd], w=[bhh])
                            S.op("dve", lambda c=c: nc.vector.tensor_tensor(
                                out=ho[:, c, q0:q0 + 512], in0=hh[:], in1=sg[:, c, q0:q0 + 512], op=ALU.mult),
                                r=[bhh, bsg], w=[bho])

                    stage1(0)
                    for i in range(len(steps)):
                        if i + 1 < len(steps):
                            stage1(i + 1)
                        stage2(i)
                    S.dma([(brT[2048 + 2 * h * 128:2048 + (2 * h + 2) * 128, :].rearrange("(c p) t -> p c t", p=128),
                            ho[:])], r=[bho])

                mprologue(0)
                for h in range(4):
                    if h + 1 < 4:
                        mprologue(h + 1)
                    mcore(h)
                S.barrier()
            if stop == "B3":
                break
            with ExitStack() as ph:
                br = sb(ph, "br", [128, 24, T], BF16)
                bbr = [S.buf("br") for _ in range(24)]
                srcb = brT.rearrange("(kt p) t -> p kt t", p=128)
                for g6 in range(6):
                    S.dma([(br[:, 4 * g6:4 * g6 + 4, :], srcb[:, 4 * g6:4 * g6 + 4, :])], w=bbr[4 * g6:4 * g6 + 4],
                          key="W_br%d" % g6)
                ws = WS
                pgb = [S.buf("pg0"), S.buf("pg1")]
                sgtr = _Ring(S, nc, ph, "sgt", 3, [128, T], BF16)
                maccr = _Ring(S, nc, ph, "macc", 2, [128, T], F32)
                mtmpr = _Ring(S, nc, ph, "mtmp", 2, [128, T], F32)
                mbfr = _Ring(S, nc, ph, "mbf", 2, [128, T], BF16)
                units = [[ws.take(("br", L, b, n)) for b in range(3)] for n in range(NT)]
                jn = 0
                for n in range(NT):
                    macc, bmacc = maccr.next()
                    for b in range(3):
                        g = jn % 2
                        jn += 1
                        gemm_job(ws, [units[n][b]], [8],
                                 lambda kg, t0, b=b: (br[:, b * 8 + kg, t0:t0 + 512], bbr[b * 8 + kg]),
                                 128, PGA[g], pgb[g])
                        sgt, bsgt = sgtr.next()
                        S.dma([(sgt[:], sgated[(b * 16 + n) * 128:(b * 16 + n + 1) * 128, :])], w=[bsgt])
                        if b == 0:
                            S.op("dve", lambda macc=macc, g=g, sgt=sgt: nc.vector.tensor_tensor(
                                out=macc[:], in0=PGA[g], in1=sgt[:], op=ALU.mult), r=[pgb[g], bsgt], w=[bmacc])
                        else:
                            mt, bmt = mtmpr.next()
                            S.op("dve", lambda mt=mt, g=g, sgt=sgt: nc.vector.tensor_tensor(
                                out=mt[:], in0=PGA[g], in1=sgt[:], op=ALU.mult), r=[pgb[g], bsgt], w=[bmt])
                            if b == 1:
                                S.op("dve", lambda macc=macc, mt=mt: nc.vector.tensor_tensor(
                                    out=macc[:], in0=macc[:], in1=mt[:], op=ALU.add), r=[bmacc, bmt], w=[bmacc])
                            else:
                                mbf, bmbf = mbfr.next()
                                S.op("dve", lambda macc=macc, mt=mt, mbf=mbf: nc.vector.tensor_tensor(
                                    out=mbf[:], in0=macc[:], in1=mt[:], op=ALU.add), r=[bmacc, bmt], w=[bmbf])
                                S.dma([(mrgd[n * 128:(n + 1) * 128, :], mbf[:])], r=[bmbf])
                S.barrier()
            if stop == "B4":
                break
            with ExitStack() as ph:
                ws = WS
                pgb = [S.buf("pg0"), S.buf("pg1")]
                units = [ws.take(("out", L, n)) for n in range(NT)]

                def alloc_b5(inner):
                    mT = sb(inner, "mT", [128, NT, T], BF16)
                    bm = [S.buf("mT") for _ in range(NT)]
                    srcm = mrgd.rearrange("(kt p) t -> p kt t", p=128)
                    for g4 in range(4):
                        S.dma([(mT[:, 4 * g4:4 * g4 + 4, :], srcm[:, 4 * g4:4 * g4 + 4, :])],
                              w=bm[4 * g4:4 * g4 + 4], key="W_mT%d" % g4)
                    return (mT, bm)

                def job_b5(ctx, n, t0, tn):
                    mT, bm = ctx
                    g = n % 2
                    gemm_job(ws, [units[n]], [16], lambda kg, tk: (mT[:, kg, tk:tk + 512], bm[kg]),
                             128, PGA[g], pgb[g])
                    return PGA[g], pgb[g]

                residual_ln(ph, x32_cur, [(0, T)], job_b5, "ln_mix_g", "ln_mix_b", L, inner_alloc=alloc_b5)
            x32_cur = x32d
            if stop == "B5":
                break
            with ExitStack() as ph:
                xbf, xb = load_xbf(ph)
                rhs_x = lambda kg, t0: (xbf[:, kg, t0:t0 + 512], xb[kg])
                ws = WS
                pgb = [S.buf("pg0"), S.buf("pg1")]
                sgfr = _Ring(S, nc, ph, "sgf", 2, [128, T], F32)
                hbr = _Ring(S, nc, ph, "hb", 2, [128, T], BF16)
                units = [(ws.take(("fg", L, f)), ws.take(("fu", L, f))) for f in range(NFT)]
                for f in range(NFT):
                    gemm_job(ws, [units[f][0]], [16], rhs_x, 128, PGA[0], pgb[0])
                    gemm_job(ws, [units[f][1]], [16], rhs_x, 128, PGA[1], pgb[1])
                    sgf, bsgf = sgfr.next()
                    S.op("act", lambda sgf=sgf: nc.scalar.activation(out=sgf[:], in_=PGA[0], func=AF.Silu),
                         r=[pgb[0]], w=[bsgf])
                    hb, bhb = hbr.next()
                    S.op("dve", lambda sgf=sgf, hb=hb: nc.vector.tensor_tensor(out=hb[:], in0=PGA[1], in1=sgf[:],
                                                                              op=ALU.mult),
                         r=[pgb[1], bsgf], w=[bhb])
                    S.dma([(ffd[f * 128:(f + 1) * 128, :], hb[:])], r=[bhb])
                S.barrier()
            if stop == "C":
                break
            with ExitStack() as ph:
                ws = WS
                pgb = [S.buf("pg0"), S.buf("pg1")]
                units = {}
                for half in range(2):
                    for n in range(NT):
                        units[(half, n)] = [ws.take(("fd", L, half, n, j)) for j in range(3)]

                def alloc_d(inner):
                    hT = sb(inner, "hT", [128, NFT, 1024], BF16)
                    bh = [S.buf("hT") for _ in range(11)]
                    return {"hT": hT, "bh": bh, "loaded": None}

                def job_d(ctx, n, t0, tn):
                    hT, bh = ctx["hT"], ctx["bh"]
                    if ctx["loaded"] != t0:
                        srch = ffd[:, t0:t0 + 1024].rearrange("(kt p) t -> p kt t", p=128)
                        for i in range(11):
                            S.dma([(hT[:, 4 * i:4 * i + 4, :], srch[:, 4 * i:4 * i + 4, :])], w=[bh[i]],
                                  key="W_hT%d" % i)
                        ctx["loaded"] = t0
                    g = n % 2
                    gemm_job(ws, units[(t0 // 1024, n)], [16, 16, 12],
                             lambda kg, tk: (hT[:, kg, tk - t0:tk - t0 + 512], bh[kg // 4]), 128, PGA[g], pgb[g],
                             tok0=t0, nch=2)
                    return PGA[g][:, 0:tn], pgb[g]

                residual_ln(ph, x32_cur, [(0, 1024), (1024, 1024)], job_d, "ln_ffn_g", "ln_ffn_b", L,
                            inner_alloc=alloc_d)
            if stop == "D":
                break
            with ExitStack() as ph:
                ws = WS
                pgb = [S.buf("pg0"), S.buf("pg1")]
                units_e = []
                for n in range(NT):
                    units_e.append((ws.take(("pg", L, n)), ws.take(("pp", L, n))))

                def alloc_e(inner):
                    xbf, xb = load_xbf(inner)
                    pTb = sb(inner, "pTb", [128, 2, T], BF16)
                    bp = S.buf("pTb")
                    with ExitStack() as tmpst:
                        p32 = sb(tmpst, "p32", [128, 2, T], F32)
                        bp32 = S.buf("p32")
                        S.dma([(p32[:], pT_in[L].rearrange("(kt p) t -> p kt t", p=128))], w=[bp32])
                        S.op("dve", lambda: nc.vector.tensor_copy(out=pTb[:], in_=p32[:]), r=[bp32], w=[bp])
                        S.barrier()
                    sgp = sb(inner, "sgp", [128, T], F32)
                    ple = sb(inner, "ple", [128, T], F32)
                    return (xbf, xb, pTb, bp, sgp, S.buf("sgp"), ple, S.buf("ple"))

                def job_e(ctx, n, t0, tn):
                    xbf, xb, pTb, bp, sgp, bsgp, ple, bple = ctx
                    gemm_job(ws, [units_e[n][0]], [16], lambda kg, tk: (xbf[:, kg, tk:tk + 512], xb[kg]),
                             128, PGA[0], pgb[0])
                    gemm_job(ws, [units_e[n][1]], [2], lambda kg, tk: (pTb[:, kg, tk:tk + 512], bp),
                             128, PGA[1], pgb[1])
                    S.op("act", lambda: nc.scalar.activation(out=sgp[:], in_=PGA[0], func=AF.Sigmoid),
                         r=[pgb[0]], w=[bsgp])
                    S.op("dve", lambda: nc.vector.tensor_tensor(out=ple[:], in0=PGA[1], in1=sgp[:], op=ALU.mult),
                         r=[pgb[1], bsgp], w=[bple])
                    return ple[:], bple

                residual_ln(ph, x32_cur, [(0, T)], job_e, "ln_ple_g", "ln_ple_b", L, inner_alloc=alloc_e,
                            final=(L == n_layers - 1 and stop is None))
            if stop == "E":
                break
            S.barrier()
    return nc


def make_in_maps(inputs, n_cores=8):
    cvec = _pack_cvec(inputs)
    cst = _consts()
    shared = {
        "w_in": np.ascontiguousarray(inputs["w_in"], dtype=np.float32),
        "w_br_conv": np.ascontiguousarray(inputs["w_br_conv"], dtype=np.float32),
        "w_br_attn": np.ascontiguousarray(inputs["w_br_attn"], dtype=np.float32),
        "w_br_mlstm": np.ascontiguousarray(inputs["w_br_mlstm"], dtype=np.float32),
        "w_out": np.ascontiguousarray(inputs["w_out"], dtype=np.float32),
        "w_ffn_gate": np.ascontiguousarray(inputs["w_ffn_gate"], dtype=np.float32),
        "w_ffn_up": np.ascontiguousarray(inputs["w_ffn_up"], dtype=np.float32),
        "w_ffn_down": np.ascontiguousarray(inputs["w_ffn_down"], dtype=np.float32),
        "w_ple_gate": np.ascontiguousarray(inputs["w_ple_gate"], dtype=np.float32),
        "w_ple_proj": np.ascontiguousarray(inputs["w_ple_proj"], dtype=np.float32),
        "cvec": cvec, "cst": cst,
    }
    maps = []
    for b in range(n_cores):
        m = dict(shared)
        m["xT"] = np.ascontiguousarray(inputs["x"][b].T, dtype=np.float32)
        m["pT"] = np.ascontiguousarray(np.transpose(inputs["p"][:, b], (0, 2, 1)), dtype=np.float32)
        m["pos"] = np.ascontiguousarray(inputs["positions"][b].reshape(1, T), dtype=np.int32)
        maps.append(m)
    return maps


_NC_CACHE = {}


def kernel(**inputs):
    inputs = {k: np.asarray(v) for k, v in inputs.items()}
    if "nc" not in _NC_CACHE:
        _NC_CACHE["nc"] = build()
    nc = _NC_CACHE["nc"]
    maps = make_in_maps(inputs)
    res = run_bass_kernel_spmd(nc, maps, core_ids=list(range(8)))
    out = np.stack([np.ascontiguousarray(res.results[b]["outT"].T) for b in range(8)], axis=0)
    return out.astype(np.float32)
```

```python
import math
from contextlib import ExitStack

import numpy as np
import concourse.bass as bass
import concourse.mybir as mybir
from concourse.bass_utils import run_bass_kernel_spmd

F32 = mybir.dt.float32
BF16 = mybir.dt.bfloat16
I32 = mybir.dt.int32
AF = mybir.ActivationFunctionType
ALU = mybir.AluOpType
AX = mybir.AxisListType

D = 2048
T = 2048
NT = 16
FF = 5632
NFT = 44
INW = 15368
DEPTH = 2
ALPHA = (2 * DEPTH) ** 0.25
EPS = 1e-5
NEG = -30000.0
SAME_ENGINE_SYNC = True
O_CV, O_CG, O_Q, O_K, O_V, O_QKM, O_VM, O_I, O_F, O_OM, O_G = 0, 1024, 2048, 3072, 4096, 5120, 7168, 8192, 8196, 8200, 9224

CV_SEGS = [("b_cv", 8), ("b_cg", 8), ("b_q", 8), ("b_k", 8), ("b_v", 8), ("b_qkm", 16), ("b_vm", 8), ("b_i", 1),
           ("b_f", 1), ("b_om", 8), ("b_g", 48), ("conv_w", 248), ("conv_b", 8), ("cln_g", 8), ("cln_b", 8),
           ("mconv_w", 64), ("mconv_b", 16), ("ln_mix_g", 16), ("ln_mix_b", 16), ("ln_ffn_g", 16),
           ("ln_ffn_b", 16), ("ln_ple_g", 16), ("ln_ple_b", 16)]
CV_OFF = {}
_o = 0
for _n, _c in CV_SEGS:
    CV_OFF[_n] = _o
    _o += _c
CV_L = _o
C_ID, C_TRI, C_ONES, C_HSEL, C_GMASK, C_INVF, C_BSEL = 0, 128, 256, 384, 896, 960, 961
C_N = C_BSEL + 1024


def _tiles(v):
    n = v.shape[0]
    return np.ascontiguousarray(v.reshape(n // 128, 128).T)


def _pack_cvec(inp):
    out = np.zeros((128, DEPTH * CV_L), np.float32)
    for l in range(DEPTH):
        b = inp["b_in"][l]
        segs = {
            "b_cv": _tiles(b[O_CV:O_CG]), "b_cg": _tiles(b[O_CG:O_Q]), "b_q": _tiles(b[O_Q:O_K]),
            "b_k": _tiles(b[O_K:O_V]), "b_v": _tiles(b[O_V:O_QKM]), "b_qkm": _tiles(b[O_QKM:O_VM]),
            "b_vm": _tiles(b[O_VM:O_I]), "b_om": _tiles(b[O_OM:O_G]), "b_g": _tiles(b[O_G:INW]),
            "conv_b": _tiles(inp["conv_b"][l]), "cln_g": _tiles(inp["conv_ln_g"][l]),
            "cln_b": _tiles(inp["conv_ln_b"][l]), "mconv_b": _tiles(inp["mconv_b"][l]),
            "ln_mix_g": _tiles(inp["ln_mix_g"][l]), "ln_mix_b": _tiles(inp["ln_mix_b"][l]),
            "ln_ffn_g": _tiles(inp["ln_ffn_g"][l]), "ln_ffn_b": _tiles(inp["ln_ffn_b"][l]),
            "ln_ple_g": _tiles(inp["ln_ple_g"][l]), "ln_ple_b": _tiles(inp["ln_ple_b"][l]),
        }
        bi = np.zeros((128, 1), np.float32)
        bi[0:4, 0] = b[O_I:O_F]
        bf = np.zeros((128, 1), np.float32)
        bf[0:4, 0] = b[O_F:O_OM]
        segs["b_i"] = bi
        segs["b_f"] = bf
        cw = inp["conv_w"][l]
        segs["conv_w"] = np.ascontiguousarray(cw.reshape(31, 8, 128).transpose(2, 1, 0).reshape(128, 248))
        mw = inp["mconv_w"][l]
        segs["mconv_w"] = np.ascontiguousarray(mw.reshape(4, 16, 128).transpose(2, 1, 0).reshape(128, 64))
        for n, c in CV_SEGS:
            a = segs[n]
            assert a.shape == (128, c), (n, a.shape)
            out[:, l * CV_L + CV_OFF[n]: l * CV_L + CV_OFF[n] + c] = a
    return out


def _consts():
    c = np.zeros((128, C_N), np.float32)
    c[:, C_ID:C_ID + 128] = np.eye(128, dtype=np.float32)
    k = np.arange(128)[:, None]
    q = np.arange(128)[None, :]
    c[:, C_TRI:C_TRI + 128] = np.where(k <= q, 0.0, NEG)
    c[:, C_ONES:C_ONES + 128] = 1.0
    for h in range(4):
        c[h, C_HSEL + h * 128: C_HSEL + (h + 1) * 128] = 1.0
    gm = np.zeros((8, 8), np.float32)
    for qt in range(8):
        for j in range(8):
            gm[qt, j] = 0.0 if j < 4 + qt // 2 else -1e30
    c[:, C_GMASK:C_GMASK + 64] = gm.reshape(1, 64)
    invf = (500000.0 ** (-(np.arange(16, dtype=np.float32) / np.float32(16)))).astype(np.float32)
    c[0:16, C_INVF] = invf
    c[16:32, C_INVF] = invf
    for j in range(8):
        c[j, C_BSEL + j * 128: C_BSEL + (j + 1) * 128] = 1.0
    return c


class _Buf:
    __slots__ = ("name", "w", "r")

    def __init__(self, name):
        self.name = name
        self.w = None
        self.r = {}


class _Eng:
    def __init__(self, name, h, sem):
        self.name = name
        self.key = "E_" + name
        self.h = h
        self.sem = sem
        self.cnt = 0
        self.seen = {}
        self.pending = False


class _Sched:
    def __init__(self, nc, stack):
        self.nc = nc
        self.stack = stack
        self.eng = {}
        for name, h in (("pe", nc.tensor), ("act", nc.scalar), ("dve", nc.vector), ("pool", nc.gpsimd),
                        ("sp", nc.sync)):
            self.eng[name] = _Eng(name, h, stack.enter_context(nc.semaphore("sem_" + name)))
        self.bar_sem = stack.enter_context(nc.semaphore("sem_bar"))
        self.bar_n = 0
        self.dsem = {}
        self.bufs = []
        self.dma_out = {}
        self.nbuf = 0
        self.free_sems = {}
        self.nsem_alloc = 0
        self.store_q = "pool"

    def buf(self, name):
        self.nbuf += 1
        b = _Buf("%s#%d" % (name, self.nbuf))
        self.bufs.append(b)
        return b

    def _deps(self, e, r, w):
        toks = []
        for b in r:
            if b.w is not None:
                toks.append(b.w)
        for b in w:
            if b.w is not None:
                toks.append(b.w)
            toks.extend(b.r.values())
        for (k, sem, val) in toks:
            if k == e.key and (e.name == "pe" or not SAME_ENGINE_SYNC):
                continue
            if e.seen.get(k, 0) >= val:
                continue
            e.h.wait_ge(sem, val)
            e.seen[k] = val

    def op(self, en, fn, r=(), w=(), inc=True):
        e = self.eng[en]
        self._deps(e, r, w)
        ins = fn()
        if inc:
            ins.then_inc(e.sem, 1)
            e.cnt += 1
            tok = (e.key, e.sem, e.cnt)
            e.pending = False
        else:
            assert en == "pe"
            tok = (e.key, e.sem, e.cnt + 1)
            e.pending = True
        for b in r:
            b.r[e.key] = tok
        for b in w:
            b.w = tok
            b.r = {}
        return tok

    def dma(self, pairs, r=(), w=(), key=None, q=None, **kw):
        if q is None:
            q = "sp" if w else self.store_q
        e = self.eng[q]
        self._deps(e, r, w)
        if key is None:
            key = ("W_" + w[0].name) if w else ("R_" + r[0].name)
        ent = self.dsem.get(key)
        if ent is None:
            fl = self.free_sems.setdefault(q, [])
            if fl:
                ent = fl.pop()
            else:
                self.nsem_alloc += 1
                ent = [self.stack.enter_context(self.nc.semaphore("d%d" % self.nsem_alloc)), 0, q]
            self.dsem[key] = ent
        assert ent[2] == q, (key, ent[2], q)
        if ent[1] > 0 and e.seen.get(key, 0) < ent[1]:
            e.h.wait_ge(ent[0], ent[1])
            e.seen[key] = ent[1]
        for (o, i) in pairs:
            e.h.dma_start(out=o, in_=i, **kw).then_inc(ent[0], 16)
            ent[1] += 16
        tok = (key, ent[0], ent[1])
        for b in r:
            b.r[key] = tok
        for b in w:
            b.w = tok
            b.r = {}
        self.dma_out[key] = tok
        return tok

    def barrier(self):
        sp = self.eng["sp"]
        for key, (k, sem, val) in self.dma_out.items():
            if sp.seen.get(k, 0) < val:
                sp.h.wait_ge(sem, val)
                sp.seen[k] = val
        for n in ("pe", "act", "dve", "pool"):
            e = self.eng[n]
            assert not e.pending, "pending non-inc op on " + n
            if e.cnt > sp.seen.get(e.key, 0):
                sp.h.wait_ge(e.sem, e.cnt)
                sp.seen[e.key] = e.cnt
        self.bar_n += 1
        sp.h.sem_inc(self.bar_sem, 1)
        for n in ("pe", "act", "dve", "pool"):
            e = self.eng[n]
            e.h.wait_ge(self.bar_sem, self.bar_n)
            for n2 in ("pe", "act", "dve", "pool"):
                e.seen[self.eng[n2].key] = self.eng[n2].cnt
        for b in self.bufs:
            b.w = None
            b.r = {}
        self.dma_out = {}
        for key, ent in self.dsem.items():
            self.free_sems.setdefault(ent[2], []).append(ent)
        self.dsem = {}
        for e in self.eng.values():
            e.seen = {k: v for k, v in e.seen.items() if k.startswith("E_")}
        self.bufs = [b for b in self.bufs if not b.name.startswith("~")]


class _Ring:
    uid = 0

    def __init__(self, S, nc, stack, name, n, shape, dtype):
        _Ring.uid += 1
        self.t = [stack.enter_context(nc.sbuf_tensor("%s_r%d_%d" % (name, _Ring.uid, i), list(shape), dtype))
                  for i in range(n)]
        self.b = [S.buf(name) for i in range(n)]
        self.n = n
        self.i = 0

    def next(self):
        i = self.i % self.n
        self.i += 1
        return self.t[i], self.b[i]


def build(n_layers=DEPTH, stop=None, dbg=()):
    nc = bass.Bass("TRN2", target_bir_lowering=False)

    def din(name, shape, dt=F32):
        return nc.dram_tensor(name, list(shape), dt, kind="ExternalInput").ap()

    def dscr(name, shape, dt):
        kind = "ExternalOutput" if name in dbg else "Internal"
        return nc.dram_tensor(name, list(shape), dt, kind=kind).ap()

    xT_in = din("xT", [D, T])
    pT_in = din("pT", [DEPTH, 256, T])
    pos_in = din("pos", [1, T], I32)
    w_in = din("w_in", [DEPTH, D, INW])
    w_brc = din("w_br_conv", [DEPTH, 1024, D])
    w_bra = din("w_br_attn", [DEPTH, 1024, D])
    w_brm = din("w_br_mlstm", [DEPTH, 1024, D])
    w_out = din("w_out", [DEPTH, D, D])
    w_fg = din("w_ffn_gate", [DEPTH, D, FF])
    w_fu = din("w_ffn_up", [DEPTH, D, FF])
    w_fd = din("w_ffn_down", [DEPTH, FF, D])
    w_pg = din("w_ple_gate", [DEPTH, D, D])
    w_pp = din("w_ple_proj", [DEPTH, 256, D])
    cvec_in = din("cvec", [128, DEPTH * CV_L])
    cst_in = din("cst", [128, C_N])
    out_T = nc.dram_tensor("outT", [D, T], F32, kind="ExternalOutput").ap()

    xbfd = dscr("xbfd", [D, T], BF16)
    x32d = dscr("x32d", [D, T], F32)
    r32d = dscr("r32d", [D, T], F32)
    brT = dscr("brT", [3072, T], BF16)
    kTd = dscr("kTd", [1024, T], BF16)
    vTd = dscr("vTd", [1024, T], BF16)
    qTd = dscr("qTd", [1024, T], BF16)
    q32d = dscr("q32d", [1024, 1024], F32)
    qkmd = dscr("qkmd", [2048, T], BF16)
    vmd = dscr("vmd", [1024, T], BF16)
    sgod = dscr("sgod", [1024, T], BF16)
    sgated = dscr("sgated", [6144, T], BF16)
    mrgd = dscr("mrgd", [D, T], BF16)
    ffd = dscr("ffd", [FF, T], BF16)
    kmd = dscr("kmd", [128, 64], F32)
    rotd = dscr("rotd", [2, 32, T], F32)
    gated = dscr("gated", [3, 4, T], F32)

    with ExitStack() as G:
        S = _Sched(nc, G)
        _uid = [0]

        def sb(st, name, shape, dt):
            _uid[0] += 1
            return st.enter_context(nc.sbuf_tensor("%s_u%d" % (name, _uid[0]), list(shape), dt))

        cv = sb(G, "cv", [128, DEPTH * CV_L], F32)
        cst = sb(G, "cst", [128, C_N], F32)
        cbf = sb(G, "cbf", [128, 384 + 1024], BF16)
        b_cv, b_cst, b_cbf = S.buf("cv"), S.buf("cst"), S.buf("cbf")
        S.dma([(cv[:], cvec_in[:, :])], w=[b_cv])
        S.dma([(cst[:], cst_in[:, :])], w=[b_cst])
        S.op("dve", lambda: nc.vector.tensor_copy(out=cbf[:, 0:384], in_=cst[:, 0:384]), r=[b_cst], w=[b_cbf])
        S.op("dve", lambda: nc.vector.tensor_copy(out=cbf[:, 384:1408], in_=cst[:, C_BSEL:C_BSEL + 1024]),
             r=[b_cst], w=[b_cbf])
        ident_bf = cbf[:, 0:128]
        tri_bf = cbf[:, 128:256]
        ones_bf = cbf[:, 256:384]
        ident32 = cst[:, C_ID:C_ID + 128]
        tri32 = cst[:, C_TRI:C_TRI + 128]
        ones32 = cst[:, C_ONES:C_ONES + 128]
        ps = G.enter_context(nc.psum_tensor("ps", [128, 8, 512], F32))
        psf = ps[:].rearrange("p b n -> p (b n)")
        S.barrier()

        def cvc(l, name, i=0, n=1):
            o = l * CV_L + CV_OFF[name] + i
            return cv[:, o:o + n]

        CONSTS = [b_cv, b_cst, b_cbf]

        class WStream:
            def __init__(self, st, nslots=3):
                self.stg = _Ring(S, nc, st, "wst", nslots, [128, 16, 128], F32)
                self.wbf = _Ring(S, nc, st, "wbf", nslots, [128, 16, 128], BF16)
                self.units = []
                self.ld = 0
                self.cs = 0
                self.stage = {}
                self.res = {}
                self.ncast = 0
                self.keys = []
                self.cursor = 0

            def add(self, src2d, nkt, M=128, dst=None, key=None):
                self.units.append((src2d, nkt, M, dst))
                self.keys.append(key)
                return len(self.units) - 1

            def take(self, key):
                i = self.cursor
                assert self.keys[i] == key, (i, self.keys[i], key)
                self.cursor += 1
                return i

            def _load(self, u):
                src2d, nkt, M, dst = self.units[u]
                t, b = self.stg.next()
                src = src2d.rearrange("(kt p) m -> p kt m", p=128)
                pairs = []
                for k0 in range(0, nkt, 4):
                    k1 = min(nkt, k0 + 4)
                    pairs.append((t[:, k0:k1, 0:M], src[:, k0:k1, :]))
                S.dma(pairs, w=[b])
                self.stage[u] = (t, b)

            def _cast(self, u):
                src2d, nkt, M, dst = self.units[u]
                t, b = self.stage.pop(u)
                if dst is None:
                    o, ob = self.wbf.next()
                    oap = o[:, 0:nkt, 0:M]
                    self.res[u] = (o, ob)
                else:
                    oap, ob = dst
                    self.res[u] = (None, ob)
                self.ncast += 1
                if self.ncast % 2:
                    S.op("act", lambda: nc.scalar.copy(out=oap, in_=t[:, 0:nkt, 0:M]), r=[b], w=[ob])
                else:
                    S.op("dve", lambda: nc.vector.tensor_copy(out=oap, in_=t[:, 0:nkt, 0:M]), r=[b], w=[ob])

            def get(self, u):
                while self.ld < min(len(self.units), u + 3):
                    self._load(self.ld)
                    self.ld += 1
                while self.cs < min(len(self.units), u + 2):
                    self._cast(self.cs)
                    self.cs += 1
                return self.res.pop(u)

        def gemm_job(ws, units, nkts, rhs_fn, M, pgap, pgbuf, tok0=0, nch=4):
            total = sum(nkts)
            kg = 0
            for u, nkt in zip(units, nkts):
                wt, wb = ws.get(u)
                for k in range(nkt):
                    for c in range(nch):
                        rap, rb = rhs_fn(kg, tok0 + c * 512)
                        last = (kg == total - 1 and c == nch - 1)
                        S.op("pe", lambda wt=wt, k=k, c=c, rap=rap, kg=kg: nc.tensor.matmul(
                            pgap[0:M, c * 512:(c + 1) * 512], lhsT=wt[:, k, 0:M], rhs=rap,
                            start=(kg == 0), stop=(kg == total - 1)),
                            r=[wb, rb], w=[pgbuf], inc=last)
                    kg += 1

        PGA = [psf[:, 0:2048], psf[:, 2048:4096]]

        def load_xbf(st, from32=None):
            xbf = sb(st, "xbf", [128, NT, T], BF16)
            xb = [S.buf("xbf") for _ in range(NT)]
            if from32 is not None:
                with ExitStack() as tmpst:
                    ring = _Ring(S, nc, tmpst, "pre32", 2, [128, T], F32)
                    for n in range(NT):
                        t, b = ring.next()
                        S.dma([(t[:], from32[n * 128:(n + 1) * 128, :])], w=[b])
                        if n % 2:
                            S.op("act", lambda t=t, n=n: nc.scalar.copy(out=xbf[:, n, :], in_=t[:]), r=[b], w=[xb[n]])
                        else:
                            S.op("dve", lambda t=t, n=n: nc.vector.tensor_copy(out=xbf[:, n, :], in_=t[:]),
                                 r=[b], w=[xb[n]])
                    S.barrier()
                return xbf, xb
            src = xbfd.rearrange("(kt p) t -> p kt t", p=128)
            for g in range(NT):
                S.dma([(xbf[:, g:g + 1, :], src[:, g:g + 1, :])], w=xb[g:g + 1], key="W_xbf%d" % g)
            return xbf, xb

        def ln_finish(st, s1, s2, bs1, bs2, nfeat, ntok=T):
            mean = sb(st, "ln_mean", [128, ntok], F32)
            rstd = sb(st, "ln_rstd", [128, ntok], F32)
            nmr = sb(st, "ln_nmr", [128, ntok], F32)
            bm, br, bn = S.buf("ln_mean"), S.buf("ln_rstd"), S.buf("ln_nmr")
            pg = [S.buf("lnpg0"), S.buf("lnpg1")]
            nch = ntok // 512
            for i, (s, bs) in enumerate(((s1, bs1), (s2, bs2))):
                for c in range(nch):
                    S.op("pe", lambda i=i, c=c, s=s: nc.tensor.matmul(
                        PGA[i][:, c * 512:(c + 1) * 512], lhsT=ones32, rhs=s[:, c * 512:(c + 1) * 512],
                        start=True, stop=True), r=[bs] + CONSTS, w=[pg[i]], inc=(c == nch - 1))
            inv = 1.0 / nfeat
            S.op("act", lambda: nc.scalar.activation(out=mean[:], in_=PGA[0][:, 0:ntok], func=AF.Copy, scale=inv),
                 r=[pg[0]], w=[bm])
            S.op("dve", lambda: nc.vector.tensor_tensor(out=nmr[:], in0=mean[:], in1=mean[:], op=ALU.mult),
                 r=[bm], w=[bn])
            S.op("dve", lambda: nc.vector.scalar_tensor_tensor(out=rstd[:], in0=PGA[1][:, 0:ntok], scalar=inv,
                                                               in1=nmr[:], op0=ALU.mult, op1=ALU.subtract),
                 r=[pg[1], bn], w=[br])
            S.op("dve", lambda: nc.vector.tensor_scalar(out=rstd[:], in0=rstd[:], scalar1=0.0, scalar2=EPS,
                                                        op0=ALU.max, op1=ALU.add), r=[br], w=[br])
            S.op("act", lambda: nc.scalar.activation(out=rstd[:], in_=rstd[:], func=AF.Sqrt), r=[br], w=[br])
            S.op("dve", lambda: nc.vector.reciprocal(out=rstd[:], in_=rstd[:]), r=[br], w=[br])
            S.op("dve", lambda: nc.vector.scalar_tensor_tensor(out=nmr[:], in0=mean[:], scalar=-1.0, in1=rstd[:],
                                                               op0=ALU.mult, op1=ALU.mult), r=[bm, br], w=[bn])
            return rstd, nmr, br, bn

        def residual_ln(ph, x32_src, chunks, y_job, gname, bname, L, inner_alloc=None, final=False):
            rtr = _Ring(S, nc, ph, "rt", 2, [128, T], F32)
            s1 = sb(ph, "ls1", [128, T], F32)
            s2 = sb(ph, "ls2", [128, T], F32)
            bs1, bs2 = S.buf("ls1"), S.buf("ls2")
            S.op("pool", lambda: nc.gpsimd.memset(s1[:], 0.0), w=[bs1])
            S.op("pool", lambda: nc.gpsimd.memset(s2[:], 0.0), w=[bs2])
            inner = ph.enter_context(ExitStack())
            x32r = _Ring(S, nc, inner, "x32t", 2, [128, T], F32)
            sq = sb(inner, "lsq", [128, T], F32)
            bsq = S.buf("lsq")
            ctx = inner_alloc(inner) if inner_alloc is not None else None
            for (t0, tn) in chunks:
                for n in range(NT):
                    xt, bxt = x32r.next()
                    S.dma([(xt[:, 0:tn], x32_src[n * 128:(n + 1) * 128, t0:t0 + tn])], w=[bxt])
                    yap, ybuf = y_job(ctx, n, t0, tn)
                    rt, brt = rtr.next()
                    S.op("dve", lambda rt=rt, xt=xt, yap=yap, tn=tn: nc.vector.scalar_tensor_tensor(
                        out=rt[:, 0:tn], in0=xt[:, 0:tn], scalar=ALPHA, in1=yap, op0=ALU.mult, op1=ALU.add),
                        r=[bxt, ybuf], w=[brt])
                    S.op("act", lambda rt=rt, tn=tn: nc.scalar.activation(out=sq[:, 0:tn], in_=rt[:, 0:tn],
                                                                          func=AF.Square), r=[brt], w=[bsq])
                    S.op("dve", lambda rt=rt, t0=t0, tn=tn: nc.vector.tensor_tensor(
                        out=s1[:, t0:t0 + tn], in0=s1[:, t0:t0 + tn], in1=rt[:, 0:tn], op=ALU.add),
                        r=[brt, bs1], w=[bs1])
                    S.op("dve", lambda t0=t0, tn=tn: nc.vector.tensor_tensor(
                        out=s2[:, t0:t0 + tn], in0=s2[:, t0:t0 + tn], in1=sq[:, 0:tn], op=ALU.add),
                        r=[bsq, bs2], w=[bs2])
                    S.dma([(r32d[n * 128:(n + 1) * 128, t0:t0 + tn], rt[:, 0:tn])], r=[brt])
            S.barrier()
            inner.close()
            rstd, nmr, brs, bnm = ln_finish(ph, s1, s2, bs1, bs2, float(D))
            xor_ = _Ring(S, nc, ph, "xo", 3, [128, T], F32)
            xbr = _Ring(S, nc, ph, "xb16", 3, [128, T], BF16)
            rt2r = _Ring(S, nc, ph, "rt2", 4, [128, T], F32)
            for n in range(NT):
                rt, brt = rtr.next() if n % 3 == 0 else rt2r.next()
                S.dma([(rt[:], r32d[n * 128:(n + 1) * 128, :])], w=[brt])
                S.op("dve", lambda rt=rt: nc.vector.tensor_tensor(out=rt[:], in0=rt[:], in1=rstd[:], op=ALU.mult),
                     r=[brt, brs], w=[brt])
                S.op("dve", lambda rt=rt: nc.vector.tensor_tensor(out=rt[:], in0=rt[:], in1=nmr[:], op=ALU.add),
                     r=[brt, bnm], w=[brt])
                xo, bxo = xor_.next()
                S.op("act", lambda rt=rt, xo=xo, n=n: nc.scalar.activation(
                    out=xo[:], in_=rt[:], func=AF.Identity, bias=cvc(L, bname, n), scale=cvc(L, gname, n)),
                    r=[brt, b_cv], w=[bxo])
                if final:
                    S.dma([(out_T[n * 128:(n + 1) * 128, :], xo[:])], r=[bxo])
                else:
                    S.dma([(x32d[n * 128:(n + 1) * 128, :], xo[:])], r=[bxo])
                    xb16, bxb = xbr.next()
                    S.op("act", lambda xo=xo, xb16=xb16: nc.scalar.copy(out=xb16[:], in_=xo[:]),
                         r=[bxo], w=[bxb])
                    S.dma([(xbfd[n * 128:(n + 1) * 128, :], xb16[:])], r=[bxb])
            S.barrier()

        WS = WStream(G)
        wbs_all = (w_brc, w_bra, w_brm)
        for L_ in range(n_layers):
            plan = []
            for c in range(8):
                plan += [(O_CV + c * 128, 128), (O_CG + c * 128, 128)]
            for h in range(8):
                plan += [(O_K + h * 128, 128), (O_V + h * 128, 128), (O_Q + h * 128, 128)]
            plan += [(O_QKM + c * 128, 128) for c in range(16)]
            plan += [(O_VM + c * 128, 128) for c in range(8)]
            plan += [(O_OM + c * 128, 128) for c in range(8)]
            plan += [(O_I, 4), (O_F, 4)]
            plan += [(O_G + c * 128, 128) for c in range(48)]
            for (c0_, M_) in plan:
                WS.add(w_in[L_, :, c0_:c0_ + M_], 16, M_, key=("in", L_, c0_, M_))
            for n in range(NT):
                for b in range(3):
                    WS.add(wbs_all[b][L_, :, n * 128:(n + 1) * 128], 8, key=("br", L_, b, n))
            for n in range(NT):
                WS.add(w_out[L_, :, n * 128:(n + 1) * 128], 16, key=("out", L_, n))
            for f in range(NFT):
                WS.add(w_fg[L_, :, f * 128:(f + 1) * 128], 16, key=("fg", L_, f))
                WS.add(w_fu[L_, :, f * 128:(f + 1) * 128], 16, key=("fu", L_, f))
            for half in range(2):
                for n in range(NT):
                    for j, (k0, nk) in enumerate(((0, 16), (2048, 16), (4096, 12))):
                        WS.add(w_fd[L_, k0:k0 + nk * 128, n * 128:(n + 1) * 128], nk, key=("fd", L_, half, n, j))
            for n in range(NT):
                WS.add(w_pg[L_, :, n * 128:(n + 1) * 128], 16, key=("pg", L_, n))
                WS.add(w_pp[L_, :, n * 128:(n + 1) * 128], 2, key=("pp", L_, n))

        with ExitStack() as sp2:
            COS = sb(sp2, "COSp", [32, T], F32)
            SINS = sb(sp2, "SINSp", [32, T], F32)
            bcos, bsin = S.buf("COSp"), S.buf("SINSp")
            with ExitStack() as spr:
                posi = sb(spr, "posi", [32, T], I32)
                ta = sb(spr, "rta", [32, T], F32)
                tb = sb(spr, "rtb", [32, T], F32)
                ti = sb(spr, "rti", [32, T], I32)
                bpi, bta, btb, bti = S.buf("posi"), S.buf("rta"), S.buf("rtb"), S.buf("rti")
                S.dma([(posi[:], pos_in.broadcast_to([32, T]))], w=[bpi])
                S.op("dve", lambda: nc.vector.tensor_copy(out=ta[:], in_=posi[:]), r=[bpi], w=[bta])
                S.op("dve", lambda: nc.vector.tensor_scalar(
                    out=ta[:], in0=ta[:], scalar1=cst[0:32, C_INVF:C_INVF + 1], scalar2=1.0 / (2 * math.pi),
                    op0=ALU.mult, op1=ALU.mult), r=[bta, b_cst], w=[bta])
                for which, dst, bdst in (("sin", SINS, bsin), ("cos", COS, bcos)):
                    sh = 0.0 if which == "sin" else 0.25
                    S.op("dve", lambda sh=sh: nc.vector.tensor_scalar(
                        out=tb[:], in0=ta[:], scalar1=sh, scalar2=None, op0=ALU.add), r=[bta], w=[btb])
                    S.op("dve", lambda: nc.vector.tensor_copy(out=ti[:], in_=tb[:]), r=[btb], w=[bti])
                    S.op("dve", lambda dst=dst: nc.vector.tensor_copy(out=dst[:], in_=ti[:]),
                         r=[bti], w=[bdst])
                    S.op("dve", lambda dst=dst: nc.vector.tensor_tensor(
                        out=tb[:], in0=tb[:], in1=dst[:], op=ALU.subtract), r=[btb, bdst], w=[btb])
                    S.op("dve", lambda dst=dst: nc.vector.tensor_scalar(
                        out=dst[:], in0=tb[:], scalar1=0.5, scalar2=None, op0=ALU.is_gt),
                        r=[btb], w=[bdst])
                    S.op("dve", lambda dst=dst: nc.vector.tensor_tensor(
                        out=tb[:], in0=tb[:], in1=dst[:], op=ALU.subtract), r=[btb, bdst], w=[btb])
                    S.op("dve", lambda dst=dst: nc.vector.tensor_scalar(
                        out=dst[:], in0=tb[:], scalar1=-0.5, scalar2=None, op0=ALU.is_lt),
                        r=[btb], w=[bdst])
                    S.op("dve", lambda dst=dst: nc.vector.tensor_tensor(
                        out=tb[:], in0=tb[:], in1=dst[:], op=ALU.add), r=[btb, bdst], w=[btb])
                    S.op("act", lambda dst=dst: nc.scalar.activation(
                        out=dst[:], in_=tb[:], func=AF.Sin, scale=2 * math.pi * (1 - 2e-6)),
                        r=[btb], w=[bdst])
                S.op("dve", lambda: nc.vector.tensor_scalar(
                    out=SINS[0:16, :], in0=SINS[0:16, :], scalar1=-1.0, scalar2=None, op0=ALU.mult),
                    r=[bsin], w=[bsin])
                S.dma([(rotd[0], COS[:])], r=[bcos])
            S.dma([(rotd[1], SINS[:])], r=[bsin])
            S.barrier()

        x32_cur = xT_in

        for L in range(n_layers):
            with ExitStack() as ph:
                xbf, xb = load_xbf(ph, from32=(xT_in if L == 0 else None))
                rhs_x = lambda kg, t0: (xbf[:, kg, t0:t0 + 512], xb[kg])
                ws = WS
                pgb = [S.buf("pg0"), S.buf("pg1")]

                def wcol(c0, M=128):
                    return ws.take(("in", L, c0, M))

                with ExitStack() as sp1:
                    units = []
                    for c in range(8):
                        units.append((wcol(O_CV + c * 128), wcol(O_CG + c * 128)))
                    ytr = _Ring(S, nc, sp1, "yt", 3, [128, T], F32)
                    s1 = sb(sp1, "s1", [128, T], F32)
                    s2 = sb(sp1, "s2", [128, T], F32)
                    bs1, bs2 = S.buf("s1"), S.buf("s2")
                    sp1i = sp1.enter_context(ExitStack())
                    sgr = _Ring(S, nc, sp1i, "sg", 2, [128, T], F32)
                    ur = _Ring(S, nc, sp1i, "ubf", 2, [128, 30 + T], BF16)
                    dgr = _Ring(S, nc, sp1i, "dg", 2, [128, 31, 128], BF16)
                    sq = sb(sp1i, "sq", [128, T], F32)
                    bsq = S.buf("sq")
                    for i in range(2):
                        S.op("pool", lambda i=i: nc.gpsimd.memset(ur.t[i][:, 0:30], 0.0), w=[ur.b[i]])
                    S.op("pool", lambda: nc.gpsimd.memset(s1[:], 0.0), w=[bs1])
                    S.op("pool", lambda: nc.gpsimd.memset(s2[:], 0.0), w=[bs2])
                    for c in range(8):
                        uv, ug = units[c]
                        gemm_job(ws, [uv], [16], rhs_x, 128, PGA[0], pgb[0])
                        gemm_job(ws, [ug], [16], rhs_x, 128, PGA[1], pgb[1])
                        sg, bsg = sgr.next()
                        S.op("act", lambda sg=sg, c=c: nc.scalar.activation(
                            out=sg[:], in_=PGA[1], func=AF.Sigmoid, bias=cvc(L, "b_cg", c), scale=1.0),
                            r=[pgb[1], b_cv], w=[bsg])
                        u, bu = ur.next()
                        S.op("dve", lambda u=u, sg=sg, c=c: nc.vector.scalar_tensor_tensor(
                            out=u[:, 30:30 + T], in0=PGA[0], scalar=cvc(L, "b_cv", c), in1=sg[:],
                            op0=ALU.add, op1=ALU.mult), r=[pgb[0], bsg, b_cv], w=[bu])
                        dg, bdg = dgr.next()
                        S.op("dve", lambda dg=dg, c=c: nc.vector.tensor_tensor(
                            out=dg[:], in0=ident_bf.unsqueeze(1).to_broadcast([128, 31, 128]),
                            in1=cvc(L, "conv_w", c * 31, 31).unsqueeze(2).to_broadcast([128, 31, 128]),
                            op=ALU.mult), r=[b_cbf, b_cv], w=[bdg])
                        for j in range(31):
                            for ch in range(4):
                                S.op("pe", lambda dg=dg, u=u, j=j, ch=ch: nc.tensor.matmul(
                                    PGA[1][:, ch * 512:(ch + 1) * 512], lhsT=dg[:, j, :],
                                    rhs=u[:, ch * 512 + j: ch * 512 + j + 512], start=(j == 0), stop=(j == 30)),
                                    r=[bdg, bu], w=[pgb[1]], inc=(j == 30 and ch == 3))
                        yt, byt = ytr.next()
                        S.op("act", lambda yt=yt, c=c: nc.scalar.activation(
                            out=yt[:], in_=PGA[1], func=AF.Identity, bias=cvc(L, "conv_b", c), scale=1.0),
                            r=[pgb[1], b_cv], w=[byt])
                        S.op("act", lambda yt=yt: nc.scalar.activation(out=sq[:], in_=yt[:], func=AF.Square),
                             r=[byt], w=[bsq])
                        S.op("dve", lambda yt=yt: nc.vector.tensor_tensor(out=s1[:], in0=s1[:], in1=yt[:], op=ALU.add),
                             r=[byt, bs1], w=[bs1])
                        S.op("dve", lambda: nc.vector.tensor_tensor(out=s2[:], in0=s2[:], in1=sq[:], op=ALU.add),
                             r=[bsq, bs2], w=[bs2])
                        S.dma([(r32d[c * 128:(c + 1) * 128, :], yt[:])], r=[byt])
                    S.barrier()
                    sp1i.close()
                    rstd, nmr, brs, bnm = ln_finish(sp1, s1, s2, bs1, bs2, 1024.0)
                    ucr = _Ring(S, nc, sp1, "ucbf", 2, [128, T], BF16)
                    for c in range(8):
                        yt, byt = ytr.next()
                        S.dma([(yt[:], r32d[c * 128:(c + 1) * 128, :])], w=[byt])
                        S.op("dve", lambda yt=yt: nc.vector.tensor_tensor(out=yt[:], in0=yt[:], in1=rstd[:],
                                                                          op=ALU.mult), r=[byt, brs], w=[byt])
                        S.op("dve", lambda yt=yt: nc.vector.tensor_tensor(out=yt[:], in0=yt[:], in1=nmr[:],
                                                                          op=ALU.add), r=[byt, bnm], w=[byt])
                        uc, buc = ucr.next()
                        S.op("act", lambda yt=yt, uc=uc, c=c: nc.scalar.activation(
                            out=uc[:], in_=yt[:], func=AF.Silu, bias=cvc(L, "cln_b", c), scale=cvc(L, "cln_g", c)),
                            r=[byt, b_cv], w=[buc])
                        S.dma([(brT[c * 128:(c + 1) * 128, :], uc[:])], r=[buc])
                    S.barrier()
                if stop == "A1":
                    break
                with ExitStack() as sp2:
                    COS = sb(sp2, "COS", [32, T], F32)
                    SINS = sb(sp2, "SINS", [32, T], F32)
                    bcos, bsin = S.buf("COS"), S.buf("SINS")
                    kmean = sb(sp2, "kmean", [128, 64], F32)
                    bkm = S.buf("kmean")
                    S.dma([(COS[:], rotd[0])], w=[bcos])
                    S.dma([(SINS[:], rotd[1])], w=[bsin])
                    t32r = _Ring(S, nc, sp2, "t32", 2, [128, T], F32)
                    swr = _Ring(S, nc, sp2, "sw", 2, [32, T], F32)
                    bfr = _Ring(S, nc, sp2, "a2bf", 3, [128, T], BF16)
                    jn = 0
                    for h in range(8):
                        uk = wcol(O_K + h * 128)
                        uvv = wcol(O_V + h * 128)
                        uq = wcol(O_Q + h * 128)
                        for kind, u in (("k", uk), ("v", uvv), ("q", uq)):
                            g = jn % 2
                            jn += 1
                            gemm_job(ws, [u], [16], rhs_x, 128, PGA[g], pgb[g])
                            o, ob = bfr.next()
                            if kind == "v":
                                S.op("act", lambda o=o, g=g, h=h: nc.scalar.activation(
                                    out=o[:], in_=PGA[g], func=AF.Identity, bias=cvc(L, "b_v", h), scale=1.0),
                                    r=[pgb[g], b_cv], w=[ob])
                                S.dma([(vTd[h * 128:(h + 1) * 128, :], o[:])], r=[ob])
                                continue
                            t, bt = t32r.next()
                            S.op("act", lambda t=t, g=g, h=h, kind=kind: nc.scalar.activation(
                                out=t[:], in_=PGA[g], func=AF.Identity, bias=cvc(L, "b_" + kind, h), scale=1.0),
                                r=[pgb[g], b_cv], w=[bt])
                            sw, bsw = swr.next()
                            S.dma([(sw[0:16, :], t[16:32, :]), (sw[16:32, :], t[0:16, :])], r=[bt], w=[bsw])
                            S.op("dve", lambda t=t: nc.vector.tensor_tensor(
                                out=t[0:32, :], in0=t[0:32, :], in1=COS[:], op=ALU.mult), r=[bt, bcos], w=[bt])
                            S.op("dve", lambda sw=sw: nc.vector.tensor_tensor(
                                out=sw[:], in0=sw[:], in1=SINS[:], op=ALU.mult), r=[bsw, bsin], w=[bsw])
                            S.op("dve", lambda t=t, sw=sw: nc.vector.tensor_tensor(
                                out=t[0:32, :], in0=t[0:32, :], in1=sw[:], op=ALU.add), r=[bt, bsw], w=[bt])
                            S.op("dve", lambda t=t, o=o: nc.vector.tensor_copy(out=o[:], in_=t[:]), r=[bt], w=[ob])
                            if kind == "k":
                                S.op("dve", lambda t=t, h=h: nc.vector.tensor_reduce(
                                    out=kmean[:, h * 8:(h + 1) * 8], in_=t[:].rearrange("p (b k) -> p b k", k=256),
                                    axis=AX.X, op=ALU.add), r=[bt], w=[bkm])
                                S.dma([(kTd[h * 128:(h + 1) * 128, :], o[:])], r=[ob])
                            else:
                                S.dma([(qTd[h * 128:(h + 1) * 128, :], o[:])], r=[ob])
                                S.dma([(q32d[h * 128:(h + 1) * 128, :], t[:, 1024:2048])], r=[bt])
                    S.dma([(kmd[:, :], kmean[:])], r=[bkm])
                    S.barrier()
                if stop == "A2":
                    break
                with ExitStack() as sp3:
                    prer = _Ring(S, nc, sp3, "mpre", 2, [128, 3 + T], F32)
                    accr = _Ring(S, nc, sp3, "macc", 2, [128, T], F32)
                    bfr = _Ring(S, nc, sp3, "a3bf", 3, [128, T], BF16)
                    for i in range(2):
                        S.op("pool", lambda i=i: nc.gpsimd.memset(prer.t[i][:, 0:3], 0.0), w=[prer.b[i]])
                    jn = 0
                    for c in range(16):
                        u = wcol(O_QKM + c * 128)
                        g = jn % 2
                        jn += 1
                        gemm_job(ws, [u], [16], rhs_x, 128, PGA[g], pgb[g])
                        pre, bpre = prer.next()
                        S.op("act", lambda pre=pre, g=g, c=c: nc.scalar.activation(
                            out=pre[:, 3:3 + T], in_=PGA[g], func=AF.Identity, bias=cvc(L, "b_qkm", c), scale=1.0),
                            r=[pgb[g], b_cv], w=[bpre])
                        acc, bacc = accr.next()
                        S.op("dve", lambda pre=pre, acc=acc, c=c: nc.vector.tensor_scalar(
                            out=acc[:], in0=pre[:, 0:T], scalar1=cvc(L, "mconv_w", c * 4), scalar2=cvc(L, "mconv_b", c),
                            op0=ALU.mult, op1=ALU.add), r=[bpre, b_cv], w=[bacc])
                        for j in range(1, 4):
                            S.op("dve", lambda pre=pre, acc=acc, c=c, j=j: nc.vector.scalar_tensor_tensor(
                                out=acc[:], in0=pre[:, j:j + T], scalar=cvc(L, "mconv_w", c * 4 + j), in1=acc[:],
                                op0=ALU.mult, op1=ALU.add), r=[bpre, bacc, b_cv], w=[bacc])
                        o, ob = bfr.next()
                        S.op("act", lambda o=o, acc=acc: nc.scalar.activation(out=o[:], in_=acc[:], func=AF.Silu),
                             r=[bacc], w=[ob])
                        S.dma([(qkmd[c * 128:(c + 1) * 128, :], o[:])], r=[ob])
                    for kind, off, bname, fn, dst in (("vm", O_VM, "b_vm", AF.Identity, vmd),
                                                     ("om", O_OM, "b_om", AF.Sigmoid, sgod)):
                        for c in range(8):
                            u = wcol(off + c * 128)
                            g = jn % 2
                            jn += 1
                            gemm_job(ws, [u], [16], rhs_x, 128, PGA[g], pgb[g])
                            o, ob = bfr.next()
                            S.op("act", lambda o=o, g=g, c=c, bname=bname, fn=fn: nc.scalar.activation(
                                out=o[:], in_=PGA[g], func=fn, bias=cvc(L, bname, c), scale=1.0),
                                r=[pgb[g], b_cv], w=[ob])
                            S.dma([(dst[c * 128:(c + 1) * 128, :], o[:])], r=[ob])
                    ig = sb(sp3, "g_ig", [4, T], F32)
                    lfm = sb(sp3, "g_lfm", [4, T], F32)
                    csm = sb(sp3, "g_cs", [4, T], F32)
                    ga = sb(sp3, "g_a", [4, T], F32)
                    gA = sb(sp3, "g_A", [4, T], F32)
                    gone = sb(sp3, "g_one", [4, T], F32)
                    nbf = sb(sp3, "g_nbf", [4, 1], F32)
                    big, blfm, bcs, bga, bgA, bone, bnbf = [S.buf("g") for _ in range(7)]
                    S.op("pool", lambda: nc.gpsimd.memset(gone[:], 1.0), w=[bone])
                    S.op("dve", lambda: nc.vector.tensor_scalar(out=nbf[:], in0=cvc(L, "b_f")[0:4, :], scalar1=-1.0,
                                                                scalar2=None, op0=ALU.mult), r=[b_cv], w=[bnbf])
                    ui = wcol(O_I, 4)
                    uf = wcol(O_F, 4)
                    gi = jn % 2
                    jn += 1
                    gemm_job(ws, [ui], [16], rhs_x, 4, PGA[gi], pgb[gi])
                    gf = jn % 2
                    jn += 1
                    gemm_job(ws, [uf], [16], rhs_x, 4, PGA[gf], pgb[gf])
                    S.op("act", lambda: nc.scalar.activation(out=ig[:], in_=PGA[gi][0:4, :], func=AF.Identity,
                                                             bias=cvc(L, "b_i")[0:4, :], scale=1.0),
                         r=[pgb[gi], b_cv], w=[big])
                    S.op("act", lambda: nc.scalar.activation(out=lfm[:], in_=PGA[gf][0:4, :], func=AF.Exp,
                                                             bias=nbf[:], scale=-1.0), r=[pgb[gf], bnbf], w=[blfm])
                    S.op("act", lambda: nc.scalar.activation(out=lfm[:], in_=lfm[:], func=AF.Ln,
                                                             bias=ones32[0:4, 0:1], scale=1.0),
                         r=[blfm, b_cst], w=[blfm])
                    S.op("dve", lambda: nc.vector.tensor_tensor_scan(out=csm[:], data0=gone[:], data1=lfm[:],
                                                                     initial=0.0, op0=ALU.mult, op1=ALU.add),
                         r=[bone, blfm], w=[bcs])
                    S.op("dve", lambda: nc.vector.tensor_tensor(out=ga[:], in0=ig[:], in1=csm[:], op=ALU.add),
                         r=[big, bcs], w=[bga])
                    S.op("dve", lambda: nc.vector.tensor_tensor_scan(out=gA[:], data0=ga[:], data1=ga[:],
                                                                     initial=0.0, op0=ALU.max, op1=ALU.max),
                         r=[bga], w=[bgA])
                    S.op("dve", lambda: nc.vector.tensor_tensor(out=csm[:], in0=csm[:], in1=gA[:], op=ALU.subtract),
                         r=[bcs, bgA], w=[bcs])
                    S.op("act", lambda: nc.scalar.activation(out=csm[:], in_=csm[:], func=AF.Exp), r=[bcs], w=[bcs])
                    S.op("dve", lambda: nc.vector.tensor_scalar(out=gA[:], in0=gA[:], scalar1=-1.0, scalar2=None,
                                                                op0=ALU.mult), r=[bgA], w=[bgA])
                    S.dma([(gated[0], ga[:])], r=[bga])
                    S.dma([(gated[1], gA[:])], r=[bgA])
                    S.dma([(gated[2], csm[:])], r=[bcs])
                    S.barrier()
                if stop == "A3":
                    break
                with ExitStack() as sp4:
                    bfr = _Ring(S, nc, sp4, "a4bf", 3, [128, T], BF16)
                    for c in range(48):
                        u = wcol(O_G + c * 128)
                        g = c % 2
                        gemm_job(ws, [u], [16], rhs_x, 128, PGA[g], pgb[g])
                        o, ob = bfr.next()
                        S.op("act", lambda o=o, g=g, c=c: nc.scalar.activation(
                            out=o[:], in_=PGA[g], func=AF.Sigmoid, bias=cvc(L, "b_g", c), scale=1.0),
                            r=[pgb[g], b_cv], w=[ob])
                        S.dma([(sgated[c * 128:(c + 1) * 128, :], o[:])], r=[ob])
                    S.barrier()
            if stop in ("A1", "A2", "A3", "A4"):
                break
            with ExitStack() as ph:
                qr = _Ring(S, nc, ph, "qT", 2, [128, T], BF16)
                kr = _Ring(S, nc, ph, "kT", 2, [128, T], BF16)
                vr = _Ring(S, nc, ph, "vT", 2, [128, T], BF16)
                q32r = _Ring(S, nc, ph, "q32g", 2, [128, 1024], F32)
                Vr = _Ring(S, nc, ph, "Vtok", 2, [128, 16, 128], BF16)
                btr = _Ring(S, nc, ph, "biasT", 2, [8, 1024], BF16)
                otr = _Ring(S, nc, ph, "oT", 2, [128, T], BF16)
                ptr = _Ring(S, nc, ph, "PT", 4, [128, 512], BF16)
                rir = _Ring(S, nc, ph, "rinv", 2, [128, 512], F32)
                gsm = [sb(ph, "gs%d" % i, [128, 64], F32) for i in range(4)]
                bgs = [S.buf("gs") for _ in range(4)]
                mm = sb(ph, "gmax", [128, 8], F32)
                bmm = S.buf("gmax")
                kmean = sb(ph, "kmeanB", [128, 64], F32)
                bkm = S.buf("kmeanB")
                S.dma([(kmean[:], kmd[:, :])], w=[bkm])
                pTPb = S.buf("pTP")
                pS = [S.buf("pS0"), S.buf("pS1"), pTPb]
                SBANK = [0, 1, 6]
                pO = [S.buf("pO0"), S.buf("pO1")]
                pR = [S.buf("pR0"), S.buf("pR1")]
                pTP = [pTPb, pTPb]
                pG = S.buf("pG7")
                pBT = pG
                SCL = 128.0 ** -0.5
                heads = {}

                def prologue1(h):
                    q, bq_ = qr.next()
                    k, bk_ = kr.next()
                    v, bv_ = vr.next()
                    q32, bq32 = q32r.next()
                    S.dma([(q[:], qTd[h * 128:(h + 1) * 128, :])], w=[bq_])
                    S.dma([(k[:], kTd[h * 128:(h + 1) * 128, :])], w=[bk_])
                    S.dma([(v[:], vTd[h * 128:(h + 1) * 128, :])], w=[bv_])
                    S.dma([(q32[:], q32d[h * 128:(h + 1) * 128, :])], w=[bq32])
                    V, bV = Vr.next()
                    for half in range(2):
                        tpv = ps[:, 6, :].bitcast(BF16)
                        for i in range(8):
                            tt = half * 8 + i
                            S.op("pe", lambda tpv=tpv, i=i, tt=tt, v=v: nc.tensor.transpose(
                                out=tpv[:, i * 128:(i + 1) * 128], in_=v[:, tt * 128:(tt + 1) * 128],
                                identity=ident_bf), r=[bv_, b_cbf], w=[pTP[half]], inc=(i == 7))
                        S.op("act", lambda tpv=tpv, half=half, V=V: nc.scalar.copy(
                            out=V[:, half * 8:(half + 1) * 8, :],
                            in_=tpv[:, :].rearrange("p (a b) -> p a b", b=128)), r=[pTP[half]], w=[bV])
                    for qt in range(8):
                        S.op("pe", lambda qt=qt, q32=q32, h=h: nc.tensor.matmul(
                            ps[:, 7, qt * 8:(qt + 1) * 8], lhsT=q32[:, qt * 128:(qt + 1) * 128],
                            rhs=kmean[:, h * 8:(h + 1) * 8], start=True, stop=True),
                            r=[bq32, bkm], w=[pG], inc=(qt == 7))
                    g, e, g2, bq = gsm
                    bg, be, bg2, bbq = bgs
                    g3 = lambda t: t[:].rearrange("p (a b) -> p a b", b=8)
                    mb = mm[:].unsqueeze(2).to_broadcast([128, 8, 8])
                    S.op("dve", lambda: nc.vector.tensor_tensor(out=g[:], in0=ps[:, 7, 0:64],
                                                                in1=cst[:, C_GMASK:C_GMASK + 64], op=ALU.add),
                         r=[pG, b_cst], w=[bg])
                    src, bsrc = g, bg
                    for it in range(2):
                        S.op("dve", lambda src=src: nc.vector.tensor_reduce(out=mm[:], in_=g3(src), axis=AX.X,
                                                                            op=ALU.max), r=[bsrc], w=[bmm])
                        S.op("dve", lambda src=src: nc.vector.tensor_tensor(out=g3(e), in0=g3(src), in1=mb,
                                                                            op=ALU.is_ge), r=[bsrc, bmm], w=[be])
                        S.op("dve", lambda src=src: nc.vector.scalar_tensor_tensor(
                            out=g2[:], in0=e[:], scalar=-1e30, in1=src[:], op0=ALU.mult, op1=ALU.add),
                            r=[be, bsrc], w=[bg2])
                        src, bsrc = g2, bg2
                    S.op("dve", lambda: nc.vector.tensor_reduce(out=mm[:], in_=g3(g2), axis=AX.X, op=ALU.max),
                         r=[bg2], w=[bmm])
                    S.op("dve", lambda: nc.vector.tensor_tensor(out=g3(e), in0=g3(g), in1=mb, op=ALU.is_lt),
                         r=[bg, bmm], w=[be])
                    S.op("dve", lambda: nc.vector.tensor_scalar(out=bq[:], in0=e[:], scalar1=NEG, scalar2=None,
                                                                op0=ALU.mult), r=[be], w=[bbq])
                    heads[h] = (q, bq_, k, bk_, V, bV)

                def prologue2(h):
                    bq, bbq = gsm[3], bgs[3]
                    bT, bbT = btr.next()
                    for half in range(2):
                        for i in range(4):
                            qt = half * 4 + i
                            S.op("pe", lambda i=i, qt=qt: nc.tensor.transpose(
                                out=ps[0:8, 7, i * 128:(i + 1) * 128], in_=bq[:, qt * 8:(qt + 1) * 8],
                                identity=ident32), r=[bbq, b_cst], w=[pBT], inc=(i == 3))
                        S.op("act", lambda half=half, bT=bT: nc.scalar.copy(
                            out=bT[:, half * 512:(half + 1) * 512], in_=ps[0:8, 7, :]), r=[pBT], w=[bbT])
                    heads[h] = heads[h] + (bT, bbT)

                def core(h):
                    q, bq_, k, bk_, V, bV, bT, bbT = heads.pop(h)
                    oT, boT = otr.next()
                    steps = []
                    for qg in range(4):
                        nkt = 4 * qg + 4
                        for kt in range(nkt):
                            steps.append((qg, kt, nkt))
                    st = {}

                    def stage1(i):
                        qg, kt, nkt = steps[i]
                        kb = kt // 2
                        segs = []
                        for s_ in range(4):
                            qsub = 4 * qg + s_
                            qblk = qsub // 2
                            if kb > qblk or (kb == qblk and kt > qsub):
                                segs.append(None)
                            elif kb == qblk:
                                segs.append(("diag" if kt == qsub else "full", False))
                            else:
                                segs.append(("full", qblk >= 4))
                        c0 = min(j for j in range(4) if segs[j] is not None) * 128
                        cb = [j for j in range(4) if segs[j] is not None and segs[j][1]]
                        cd = [j for j in range(4) if segs[j] is not None and segs[j][0] == "diag"]
                        g = i % 3
                        Sb = ps[:, SBANK[g], :]
                        nextra = (1 if cb else 0) + len(cd)
                        S.op("pe", lambda: nc.tensor.matmul(
                            Sb[:, c0:512], lhsT=k[:, kt * 128:(kt + 1) * 128],
                            rhs=q[:, qg * 512 + c0:qg * 512 + 512], start=True, stop=(nextra == 0)),
                            r=[bk_, bq_], w=[pS[g]], inc=(nextra == 0))
                        if cb:
                            b0 = min(cb) * 128
                            nextra -= 1
                            S.op("pe", lambda nextra=nextra: nc.tensor.matmul(
                                Sb[:, b0:512], lhsT=cbf[0:8, 384 + kb * 128:384 + (kb + 1) * 128],
                                rhs=bT[0:8, qg * 512 + b0 - 1024:qg * 512 + 512 - 1024],
                                start=False, stop=(nextra == 0)),
                                r=[bbT, b_cbf], w=[pS[g]], inc=(nextra == 0))
                        for j in cd:
                            nextra -= 1
                            S.op("pe", lambda j=j, nextra=nextra: nc.tensor.matmul(
                                Sb[:, j * 128:(j + 1) * 128], lhsT=ident_bf, rhs=tri_bf,
                                start=False, stop=(nextra == 0)),
                                r=[b_cbf], w=[pS[g]], inc=(nextra == 0))
                        PT, bPT = ptr.next()
                        S.op("act", lambda: nc.scalar.activation(
                            out=PT[:, c0:512], in_=Sb[:, c0:512], func=AF.Exp, scale=SCL),
                            r=[pS[g]], w=[bPT])
                        st[i] = (PT, bPT, c0)

                    def stage2(i):
                        qg, kt, nkt = steps[i]
                        PT, bPT, c0 = st.pop(i)
                        a = qg % 2
                        S.op("pe", lambda: nc.tensor.matmul(
                            ps[:, 2 + 2 * a, c0:512], lhsT=V[:, kt, :], rhs=PT[:, c0:512],
                            start=(kt == 0), stop=(kt == nkt - 1)), r=[bV, bPT], w=[pO[a]], inc=(kt == nkt - 1))
                        S.op("pe", lambda: nc.tensor.matmul(
                            ps[:, 3 + 2 * a, c0:512], lhsT=ones_bf, rhs=PT[:, c0:512],
                            start=(kt == 0), stop=(kt == nkt - 1)), r=[b_cbf, bPT], w=[pR[a]], inc=(kt == nkt - 1))
                        if kt == nkt - 1:
                            ri, bri = rir.next()
                            S.op("dve", lambda: nc.vector.reciprocal(out=ri[:], in_=ps[:, 3 + 2 * a, :]),
                                 r=[pR[a]], w=[bri])
                            S.op("dve", lambda: nc.vector.tensor_tensor(
                                out=oT[:, qg * 512:(qg + 1) * 512], in0=ps[:, 2 + 2 * a, :], in1=ri[:], op=ALU.mult),
                                r=[pO[a], bri], w=[boT])

                    stage1(0)
                    stage1(1)
                    for i in range(len(steps)):
                        if i + 2 < len(steps):
                            stage1(i + 2)
                        stage2(i)
                    S.dma([(brT[1024 + h * 128:1024 + (h + 1) * 128, :], oT[:])], r=[boT])

                prologue1(0)
                prologue2(0)
                for h in range(8):
                    if h + 1 < 8:
                        prologue1(h + 1)
                    core(h)
                    if h + 1 < 8:
                        prologue2(h + 1)
                S.barrier()
            if stop == "B2":
                break
            with ExitStack() as ph:
                a4 = sb(ph, "m_a4", [4, T], F32)
                negA = sb(ph, "m_negA", [4, T], F32)
                em = sb(ph, "m_em", [4, T], F32)
                acol = sb(ph, "m_acol", [128, 64], F32)
                ba4, bnA, bem, bacol = [S.buf("mg") for _ in range(4)]
                S.dma([(a4[:], gated[0])], w=[ba4])
                S.dma([(negA[:], gated[1])], w=[bnA])
                S.dma([(em[:], gated[2])], w=[bem])
                nA2 = sb(ph, "m_nA2", [36, T], BF16)
                hs36 = sb(ph, "m_hs36", [36, 512], BF16)
                em2 = sb(ph, "m_em2", [36, T], BF16)
                bnA2, bhs, bem2 = S.buf("m_nA2"), S.buf("m_hs36"), S.buf("m_em2")
                with ExitStack() as tmps:
                    hi32 = sb(tmps, "m_hi32", [4, T], F32)
                    lo16 = sb(tmps, "m_lo16", [4, T], BF16)
                    bhi, blo = S.buf("m_hi32"), S.buf("m_lo16")
                    S.op("pool", lambda: nc.gpsimd.memset(nA2[:], 0.0), w=[bnA2])
                    S.op("pool", lambda: nc.gpsimd.memset(hs36[:], 0.0), w=[bhs])
                    S.op("dve", lambda: nc.vector.tensor_copy(out=nA2[0:4, :], in_=negA[:]), r=[bnA], w=[bnA2])
                    S.op("dve", lambda: nc.vector.tensor_copy(out=hi32[:], in_=nA2[0:4, :]), r=[bnA2], w=[bhi])
                    S.op("dve", lambda: nc.vector.tensor_tensor(out=hi32[:], in0=negA[:], in1=hi32[:],
                                                                op=ALU.subtract), r=[bnA, bhi], w=[bhi])
                    S.op("dve", lambda: nc.vector.tensor_copy(out=lo16[:], in_=hi32[:]), r=[bhi], w=[blo])
                    S.dma([(nA2[32:36, :], lo16[:])], r=[blo], w=[bnA2])
                    S.op("dve", lambda: nc.vector.tensor_copy(out=hs36[0:4, :], in_=cst[0:4, C_HSEL:C_HSEL + 512]),
                         r=[b_cst], w=[bhs])
                    S.dma([(hs36[32:36, :], hs36[0:4, :])], r=[bhs], w=[bhs])
                    S.op("pool", lambda: nc.gpsimd.memset(em2[:], 0.0), w=[bem2])
                    S.op("dve", lambda: nc.vector.tensor_copy(out=em2[0:4, :], in_=em[:]), r=[bem], w=[bem2])
                    S.op("dve", lambda: nc.vector.tensor_copy(out=hi32[:], in_=em2[0:4, :]), r=[bem2, blo], w=[bhi])
                    S.op("dve", lambda: nc.vector.tensor_tensor(out=hi32[:], in0=em[:], in1=hi32[:],
                                                                op=ALU.subtract), r=[bem, bhi], w=[bhi])
                    S.op("dve", lambda: nc.vector.tensor_copy(out=lo16[:], in_=hi32[:]), r=[bhi], w=[blo])
                    S.dma([(em2[32:36, :], lo16[:])], r=[blo], w=[bem2])
                    S.barrier()
                qmr = _Ring(S, nc, ph, "qm", 2, [128, 2, T], BF16)
                kmr = _Ring(S, nc, ph, "km", 2, [128, 2, T], BF16)
                vmr = _Ring(S, nc, ph, "vm", 2, [128, 2, T], BF16)
                sgr = _Ring(S, nc, ph, "sgo", 2, [128, 2, T], BF16)
                Vr = _Ring(S, nc, ph, "Vm", 2, [128, 16, 256], BF16)
                hor = _Ring(S, nc, ph, "hout", 1, [128, 2, T], BF16)
                dtr = _Ring(S, nc, ph, "DT", 2, [128, 512], F32)
                ptr = _Ring(S, nc, ph, "PTm", 3, [128, 512], BF16)
                evr = _Ring(S, nc, ph, "mev", 2, [128, 3, 512], F32)
                cnr = _Ring(S, nc, ph, "mcn", 2, [128, 2], F32)
                eqr = _Ring(S, nc, ph, "meq", 2, [128, 512], F32)
                ekr = _Ring(S, nc, ph, "mek", 2, [128, 16], F32)
                tmp = [sb(ph, "mtmp%d" % i, [128, 512], F32) for i in range(3)]
                btmp = [S.buf("mtmp") for _ in range(3)]
                pS = [S.buf("pS0"), S.buf("pS1")]
                pD = [S.buf("pD0"), S.buf("pD1")]
                pN = [S.buf("pN0"), S.buf("pN1"), S.buf("pDen")]
                pT7 = S.buf("pT7")
                for tt in range(16):
                    S.op("pe", lambda tt=tt: nc.tensor.transpose(
                        out=ps[:, 7, tt * 4:(tt + 1) * 4], in_=a4[0:4, tt * 128:(tt + 1) * 128],
                        identity=ident32[0:4, 0:4]), r=[ba4, b_cst], w=[pT7], inc=(tt == 15))
                S.op("act", lambda: nc.scalar.copy(out=acol[:], in_=ps[:, 7, 0:64]), r=[pT7], w=[bacol])
                mh = {}

                def mprologue(h):
                    qm, bqm = qmr.next()
                    km, bkm_ = kmr.next()
                    vm, bvm = vmr.next()
                    sg, bsg = sgr.next()
                    v3 = lambda ap_: ap_.rearrange("(c p) t -> p c t", p=128)
                    S.dma([(qm[:], v3(qkmd[2 * h * 128:(2 * h + 2) * 128, :]))], w=[bqm])
                    S.dma([(km[:], v3(qkmd[(8 + 2 * h) * 128:(10 + 2 * h) * 128, :]))], w=[bkm_])
                    S.dma([(vm[:], v3(vmd[2 * h * 128:(2 * h + 2) * 128, :]))], w=[bvm])
                    S.dma([(sg[:], v3(sgod[2 * h * 128:(2 * h + 2) * 128, :]))], w=[bsg])
                    V, bV = Vr.next()
                    tpv = ps[:, 7, :].bitcast(BF16)
                    for c in range(2):
                        for half in range(2):
                            for i in range(8):
                                tt = half * 8 + i
                                S.op("pe", lambda i=i, tt=tt, c=c, vm=vm: nc.tensor.transpose(
                                    out=tpv[:, i * 128:(i + 1) * 128], in_=vm[:, c, tt * 128:(tt + 1) * 128],
                                    identity=ident_bf), r=[bvm, b_cbf], w=[pT7], inc=(i == 7))
                            S.op("act", lambda half=half, c=c, V=V: nc.scalar.copy(
                                out=V[:, half * 8:(half + 1) * 8, c * 128:(c + 1) * 128],
                                in_=tpv[:, :].rearrange("p (a b) -> p a b", b=128)), r=[pT7], w=[bV])
                    mh[h] = (qm, bqm, km, bkm_, V, bV, sg, bsg)

                def mcore(h):
                    qm, bqm, km, bkm_, V, bV, sg, bsg = mh.pop(h)
                    ho, bho = hor.next()
                    hsel = cst[0:4, C_HSEL + h * 128:C_HSEL + (h + 1) * 128]
                    steps = []
                    for qg in range(4):
                        nkt = 4 * qg + 4
                        for kt in range(nkt):
                            steps.append((qg, kt, nkt))
                    st = {}
                    cnt = [0]

                    qst = {}

                    def qg_setup(qg):
                        q0 = qg * 512
                        g = cnt[0] % 2
                        cnt[0] += 1
                        Db = ps[:, 2 + g, :]
                        S.op("pe", lambda: nc.tensor.matmul(
                            Db[:, 0:1], lhsT=hs36[0:36, h * 128:(h + 1) * 128], rhs=nA2[0:36, q0 - 1:q0],
                            start=True, stop=True), r=[bnA2, bhs], w=[pD[g]], inc=True)
                        cn, bcn = cnr.next()
                        S.op("dve", lambda: nc.vector.tensor_scalar(
                            out=cn[:, 0:1], in0=Db[:, 0:1], scalar1=math.log(0.0625), scalar2=None, op0=ALU.add),
                            r=[pD[g]], w=[bcn])
                        S.op("dve", lambda: nc.vector.tensor_scalar(
                            out=cn[:, 1:2], in0=Db[:, 0:1], scalar1=-1.0, scalar2=None, op0=ALU.mult),
                            r=[pD[g]], w=[bcn])
                        g2 = cnt[0] % 2
                        cnt[0] += 1
                        Db2 = ps[:, 2 + g2, :]
                        S.op("pe", lambda: nc.tensor.matmul(
                            Db2[:, :], lhsT=hs36[0:36, h * 128:(h + 1) * 128], rhs=nA2[0:36, q0:q0 + 512],
                            start=True, stop=True), r=[bnA2, bhs], w=[pD[g2]], inc=True)
                        eq, beq = eqr.next()
                        S.op("act", lambda: nc.scalar.activation(
                            out=eq[:], in_=Db2[:, :], func=AF.Exp, bias=cn[:, 1:2], scale=1.0),
                            r=[pD[g2], bcn], w=[beq])
                        ek, bek = ekr.next()
                        nk = 4 * qg
                        acol3 = acol[:].rearrange("p (k hh) -> p k hh", hh=4)
                        S.op("act", lambda: nc.scalar.activation(
                            out=ek[:, 0:nk], in_=acol3[:, 0:nk, h], func=AF.Exp, bias=cn[:, 0:1], scale=1.0),
                            r=[bacol, bcn], w=[bek])
                        qst[qg] = (eq, beq, ek, bek)

                    def stage1(i):
                        qg, kt, nkt = steps[i]
                        q0 = qg * 512
                        rr = kt - 4 * qg
                        c0 = max(0, rr) * 128
                        if kt == 0 and qg >= 1:
                            qg_setup(qg)
                        g = cnt[0] % 2
                        cnt[0] += 1
                        Sb = ps[:, g, :]
                        Db = ps[:, 2 + g, :]
                        for c in range(2):
                            S.op("pe", lambda c=c: nc.tensor.matmul(
                                Sb[:, c0:512], lhsT=km[:, c, kt * 128:(kt + 1) * 128],
                                rhs=qm[:, c, q0 + c0:q0 + 512], start=(c == 0), stop=(c == 1)),
                                r=[bkm_, bqm], w=[pS[g]], inc=(c == 1))
                        if rr < 0:
                            eq, beq, ek, bek = qst[qg]
                            PT, bPT = ptr.next()
                            S.op("dve", lambda: nc.vector.scalar_tensor_tensor(
                                out=PT[:, :], in0=Sb[:, :], scalar=ek[:, kt:kt + 1], in1=eq[:, :],
                                op0=ALU.mult, op1=ALU.mult), r=[pS[g], bek, beq], w=[bPT])
                            st[i] = (PT, bPT, 0)
                            return
                        S.op("pe", lambda: nc.tensor.matmul(
                            Db[:, c0:512], lhsT=hs36[0:36, h * 128:(h + 1) * 128],
                            rhs=nA2[0:36, q0 + c0:q0 + 512],
                            start=True, stop=(rr < 0)), r=[bnA2, bhs], w=[pD[g]], inc=(rr < 0))
                        if rr >= 0:
                            S.op("pe", lambda: nc.tensor.matmul(
                                Db[:, c0:c0 + 128], lhsT=ident_bf, rhs=tri_bf, start=False, stop=True),
                                r=[b_cbf], w=[pD[g]], inc=True)
                        DT, bDT = dtr.next()
                        S.op("act", lambda: nc.scalar.activation(
                            out=DT[:, c0:512], in_=Db[:, c0:512], func=AF.Exp,
                            bias=acol[:, kt * 4 + h:kt * 4 + h + 1], scale=1.0), r=[pD[g], bacol], w=[bDT])
                        PT, bPT = ptr.next()
                        S.op("dve", lambda: nc.vector.scalar_tensor_tensor(
                            out=PT[:, c0:512], in0=Sb[:, c0:512], scalar=0.0625, in1=DT[:, c0:512],
                            op0=ALU.mult, op1=ALU.mult), r=[pS[g], bDT], w=[bPT])
                        st[i] = (PT, bPT, c0)

                    def stage2(i):
                        qg, kt, nkt = steps[i]
                        q0 = qg * 512
                        PT, bPT, c0 = st.pop(i)
                        for c in range(3):
                            lw = V[:, kt, c * 128:(c + 1) * 128] if c < 2 else ones_bf
                            S.op("pe", lambda lw=lw, c=c: nc.tensor.matmul(
                                ps[:, 4 + c, c0:512], lhsT=lw, rhs=PT[:, c0:512],
                                start=(kt == 0), stop=(kt == nkt - 1)),
                                r=[bV, bPT, b_cbf], w=[pN[c]], inc=(kt == nkt - 1))
                        if kt != nkt - 1:
                            return
                        ev, bev = evr.next()
                        for c in range(3):
                            S.op("act", lambda c=c: nc.scalar.activation(
                                out=ev[:, c, :], in_=ps[:, 4 + c, :], func=(AF.Abs if c == 2 else AF.Copy)),
                                r=[pN[c]], w=[bev])
                        g = cnt[0] % 2
                        cnt[0] += 1
                        Db = ps[:, 2 + g, :]
                        S.op("pe", lambda: nc.tensor.matmul(
                            Db[:, :], lhsT=hs36[0:36, h * 128:(h + 1) * 128], rhs=em2[0:36, q0:q0 + 512],
                            start=True, stop=True), r=[bem2, bhs], w=[pD[g]], inc=True)
                        dn, rd, hh = tmp
                        bdn, brd, bhh = btmp
                        S.op("dve", lambda: nc.vector.tensor_tensor(
                            out=dn[:], in0=ev[:, 2, :], in1=Db[:, :], op=ALU.max), r=[bev, pD[g]], w=[bdn])
                        S.op("dve", lambda: nc.vector.reciprocal(out=rd[:], in_=dn[:]), r=[bdn], w=[brd])
                        for c in range(2):
                            S.op("dve", lambda c=c: nc.vector.tensor_tensor(
                                out=hh[:], in0=ev[:, c, :], in1=rd[:], op=ALU.mult), r=[bev, brd], w=[bhh])
                            S.op("dve", lambda c=c: nc.vector.tensor_tensor(
                                out=ho[:, c, q0:q0 + 512], in0=hh[:], in1=sg[:, c, q0:q0 + 512], op=ALU.mult),
                                r=[bhh, bsg], w=[bho])

                    stage1(0)
                    for i in range(len(steps)):
                        if i + 1 < len(steps):
                            stage1(i + 1)
                        stage2(i)
                    S.dma([(brT[2048 + 2 * h * 128:2048 + (2 * h + 2) * 128, :].rearrange("(c p) t -> p c t", p=128),
                            ho[:])], r=[bho])

                mprologue(0)
                for h in range(4):
                    if h + 1 < 4:
                        mprologue(h + 1)
                    mcore(h)
                S.barrier()
            if stop == "B3":
                break
            with ExitStack() as ph:
                br = sb(ph, "br", [128, 24, T], BF16)
                bbr = [S.buf("br") for _ in range(24)]
                srcb = brT.rearrange("(kt p) t -> p kt t", p=128)
                for g6 in range(6):
                    S.dma([(br[:, 4 * g6:4 * g6 + 4, :], srcb[:, 4 * g6:4 * g6 + 4, :])], w=bbr[4 * g6:4 * g6 + 4],
                          key="W_br%d" % g6)
                ws = WS
                pgb = [S.buf("pg0"), S.buf("pg1")]
                sgtr = _Ring(S, nc, ph, "sgt", 3, [128, T], BF16)
                maccr = _Ring(S, nc, ph, "macc", 2, [128, T], F32)
                mtmpr = _Ring(S, nc, ph, "mtmp", 2, [128, T], F32)
                mbfr = _Ring(S, nc, ph, "mbf", 2, [128, T], BF16)
                units = [[ws.take(("br", L, b, n)) for b in range(3)] for n in range(NT)]
                jn = 0
                for n in range(NT):
                    macc, bmacc = maccr.next()
                    for b in range(3):
                        g = jn % 2
                        jn += 1
                        gemm_job(ws, [units[n][b]], [8],
                                 lambda kg, t0, b=b: (br[:, b * 8 + kg, t0:t0 + 512], bbr[b * 8 + kg]),
                                 128, PGA[g], pgb[g])
                        sgt, bsgt = sgtr.next()
                        S.dma([(sgt[:], sgated[(b * 16 + n) * 128:(b * 16 + n + 1) * 128, :])], w=[bsgt])
                        if b == 0:
                            S.op("dve", lambda macc=macc, g=g, sgt=sgt: nc.vector.tensor_tensor(
                                out=macc[:], in0=PGA[g], in1=sgt[:], op=ALU.mult), r=[pgb[g], bsgt], w=[bmacc])
                        else:
                            mt, bmt = mtmpr.next()
                            S.op("dve", lambda mt=mt, g=g, sgt=sgt: nc.vector.tensor_tensor(
                                out=mt[:], in0=PGA[g], in1=sgt[:], op=ALU.mult), r=[pgb[g], bsgt], w=[bmt])
                            if b == 1:
                                S.op("dve", lambda macc=macc, mt=mt: nc.vector.tensor_tensor(
                                    out=macc[:], in0=macc[:], in1=mt[:], op=ALU.add), r=[bmacc, bmt], w=[bmacc])
                            else:
                                mbf, bmbf = mbfr.next()
                                S.op("dve", lambda macc=macc, mt=mt, mbf=mbf: nc.vector.tensor_tensor(
                                    out=mbf[:], in0=macc[:], in1=mt[:], op=ALU.add), r=[bmacc, bmt], w=[bmbf])
                                S.dma([(mrgd[n * 128:(n + 1) * 128, :], mbf[:])], r=[bmbf])
                S.barrier()
            if stop == "B4":
                break
            with ExitStack() as ph:
                ws = WS
                pgb = [S.buf("pg0"), S.buf("pg1")]
                units = [ws.take(("out", L, n)) for n in range(NT)]

                def alloc_b5(inner):
                    mT = sb(inner, "mT", [128, NT, T], BF16)
                    bm = [S.buf("mT") for _ in range(NT)]
                    srcm = mrgd.rearrange("(kt p) t -> p kt t", p=128)
                    for g4 in range(4):
                        S.dma([(mT[:, 4 * g4:4 * g4 + 4, :], srcm[:, 4 * g4:4 * g4 + 4, :])],
                              w=bm[4 * g4:4 * g4 + 4], key="W_mT%d" % g4)
                    return (mT, bm)

                def job_b5(ctx, n, t0, tn):
                    mT, bm = ctx
                    g = n % 2
                    gemm_job(ws, [units[n]], [16], lambda kg, tk: (mT[:, kg, tk:tk + 512], bm[kg]),
                             128, PGA[g], pgb[g])
                    return PGA[g], pgb[g]

                residual_ln(ph, x32_cur, [(0, T)], job_b5, "ln_mix_g", "ln_mix_b", L, inner_alloc=alloc_b5)
            x32_cur = x32d
            if stop == "B5":
                break
            with ExitStack() as ph:
                xbf, xb = load_xbf(ph)
                rhs_x = lambda kg, t0: (xbf[:, kg, t0:t0 + 512], xb[kg])
                ws = WS
                pgb = [S.buf("pg0"), S.buf("pg1")]
                sgfr = _Ring(S, nc, ph, "sgf", 2, [128, T], F32)
                hbr = _Ring(S, nc, ph, "hb", 2, [128, T], BF16)
                units = [(ws.take(("fg", L, f)), ws.take(("fu", L, f))) for f in range(NFT)]
                for f in range(NFT):
                    gemm_job(ws, [units[f][0]], [16], rhs_x, 128, PGA[0], pgb[0])
                    gemm_job(ws, [units[f][1]], [16], rhs_x, 128, PGA[1], pgb[1])
                    sgf, bsgf = sgfr.next()
                    S.op("act", lambda sgf=sgf: nc.scalar.activation(out=sgf[:], in_=PGA[0], func=AF.Silu),
                         r=[pgb[0]], w=[bsgf])
                    hb, bhb = hbr.next()
                    S.op("dve", lambda sgf=sgf, hb=hb: nc.vector.tensor_tensor(out=hb[:], in0=PGA[1], in1=sgf[:],
                                                                              op=ALU.mult),
                         r=[pgb[1], bsgf], w=[bhb])
                    S.dma([(ffd[f * 128:(f + 1) * 128, :], hb[:])], r=[bhb])
                S.barrier()
            if stop == "C":
                break
            with ExitStack() as ph:
                ws = WS
                pgb = [S.buf("pg0"), S.buf("pg1")]
                units = {}
                for half in range(2):
                    for n in range(NT):
                        units[(half, n)] = [ws.take(("fd", L, half, n, j)) for j in range(3)]

                def alloc_d(inner):
                    hT = sb(inner, "hT", [128, NFT, 1024], BF16)
                    bh = [S.buf("hT") for _ in range(11)]
                    return {"hT": hT, "bh": bh, "loaded": None}

                def job_d(ctx, n, t0, tn):
                    hT, bh = ctx["hT"], ctx["bh"]
                    if ctx["loaded"] != t0:
                        srch = ffd[:, t0:t0 + 1024].rearrange("(kt p) t -> p kt t", p=128)
                        for i in range(11):
                            S.dma([(hT[:, 4 * i:4 * i + 4, :], srch[:, 4 * i:4 * i + 4, :])], w=[bh[i]],
                                  key="W_hT%d" % i)
                        ctx["loaded"] = t0
                    g = n % 2
                    gemm_job(ws, units[(t0 // 1024, n)], [16, 16, 12],
                             lambda kg, tk: (hT[:, kg, tk - t0:tk - t0 + 512], bh[kg // 4]), 128, PGA[g], pgb[g],
                             tok0=t0, nch=2)
                    return PGA[g][:, 0:tn], pgb[g]

                residual_ln(ph, x32_cur, [(0, 1024), (1024, 1024)], job_d, "ln_ffn_g", "ln_ffn_b", L,
                            inner_alloc=alloc_d)
            if stop == "D":
                break
            with ExitStack() as ph:
                ws = WS
                pgb = [S.buf("pg0"), S.buf("pg1")]
                units_e = []
                for n in range(NT):
                    units_e.append((ws.take(("pg", L, n)), ws.take(("pp", L, n))))

                def alloc_e(inner):
                    xbf, xb = load_xbf(inner)
                    pTb = sb(inner, "pTb", [128, 2, T], BF16)
                    bp = S.buf("pTb")
                    with ExitStack() as tmpst:
                        p32 = sb(tmpst, "p32", [128, 2, T], F32)
                        bp32 = S.buf("p32")
                        S.dma([(p32[:], pT_in[L].rearrange("(kt p) t -> p kt t", p=128))], w=[bp32])
                        S.op("dve", lambda: nc.vector.tensor_copy(out=pTb[:], in_=p32[:]), r=[bp32], w=[bp])
                        S.barrier()
                    sgp = sb(inner, "sgp", [128, T], F32)
                    ple = sb(inner, "ple", [128, T], F32)
                    return (xbf, xb, pTb, bp, sgp, S.buf("sgp"), ple, S.buf("ple"))

                def job_e(ctx, n, t0, tn):
                    xbf, xb, pTb, bp, sgp, bsgp, ple, bple = ctx
                    gemm_job(ws, [units_e[n][0]], [16], lambda kg, tk: (xbf[:, kg, tk:tk + 512], xb[kg]),
                             128, PGA[0], pgb[0])
                    gemm_job(ws, [units_e[n][1]], [2], lambda kg, tk: (pTb[:, kg, tk:tk + 512], bp),
                             128, PGA[1], pgb[1])
                    S.op("act", lambda: nc.scalar.activation(out=sgp[:], in_=PGA[0], func=AF.Sigmoid),
                         r=[pgb[0]], w=[bsgp])
                    S.op("dve", lambda: nc.vector.tensor_tensor(out=ple[:], in0=PGA[1], in1=sgp[:], op=ALU.mult),
                         r=[pgb[1], bsgp], w=[bple])
                    return ple[:], bple

                residual_ln(ph, x32_cur, [(0, T)], job_e, "ln_ple_g", "ln_ple_b", L, inner_alloc=alloc_e,
                            final=(L == n_layers - 1 and stop is None))
            if stop == "E":
                break
            S.barrier()
    return nc


def make_in_maps(inputs, n_cores=8):
    cvec = _pack_cvec(inputs)
    cst = _consts()
    shared = {
        "w_in": np.ascontiguousarray(inputs["w_in"], dtype=np.float32),
        "w_br_conv": np.ascontiguousarray(inputs["w_br_conv"], dtype=np.float32),
        "w_br_attn": np.ascontiguousarray(inputs["w_br_attn"], dtype=np.float32),
        "w_br_mlstm": np.ascontiguousarray(inputs["w_br_mlstm"], dtype=np.float32),
        "w_out": np.ascontiguousarray(inputs["w_out"], dtype=np.float32),
        "w_ffn_gate": np.ascontiguousarray(inputs["w_ffn_gate"], dtype=np.float32),
        "w_ffn_up": np.ascontiguousarray(inputs["w_ffn_up"], dtype=np.float32),
        "w_ffn_down": np.ascontiguousarray(inputs["w_ffn_down"], dtype=np.float32),
        "w_ple_gate": np.ascontiguousarray(inputs["w_ple_gate"], dtype=np.float32),
        "w_ple_proj": np.ascontiguousarray(inputs["w_ple_proj"], dtype=np.float32),
        "cvec": cvec, "cst": cst,
    }
    maps = []
    for b in range(n_cores):
        m = dict(shared)
        m["xT"] = np.ascontiguousarray(inputs["x"][b].T, dtype=np.float32)
        m["pT"] = np.ascontiguousarray(np.transpose(inputs["p"][:, b], (0, 2, 1)), dtype=np.float32)
        m["pos"] = np.ascontiguousarray(inputs["positions"][b].reshape(1, T), dtype=np.int32)
        maps.append(m)
    return maps


_NC_CACHE = {}


def kernel(**inputs):
    inputs = {k: np.asarray(v) for k, v in inputs.items()}
    if "nc" not in _NC_CACHE:
        _NC_CACHE["nc"] = build()
    nc = _NC_CACHE["nc"]
    maps = make_in_maps(inputs)
    res = run_bass_kernel_spmd(nc, maps, core_ids=list(range(8)))
    out = np.stack([np.ascontiguousarray(res.results[b]["outT"].T) for b in range(8)], axis=0)
    return out.astype(np.float32)
```
